# Optimizing a Trainium2 kernel written in Bass

```python
import jax, jax.numpy as jnp
from jax import lax
import numpy as np

D_MODEL = 1024
BATCH = 8
SEQ = 4096
DEPTH = 4

CTX_LEN = 256
GRID_W = 64

GLA_HEADS = 4
GLA_KEY = D_MODEL // 2
GLA_VAL = D_MODEL
GLA_DK = GLA_KEY // GLA_HEADS
GLA_DV = GLA_VAL // GLA_HEADS
GLA_GATE_RANK = 16
GLA_TAU = 16.0
GLA_CHUNK = 64

MLA_HEADS = D_MODEL // 128
MLA_Q_RANK = 3 * D_MODEL // 8
MLA_KV_RANK = D_MODEL // 4
MLA_NOPE = 128
MLA_ROPE = 64
MLA_V = 128
MLA_SCALE = (MLA_NOPE + MLA_ROPE) ** -0.5
ROPE_F = MLA_ROPE // 4
ROPE_BASE = 10000.0
Q_BLOCK = 128

CONV_CH = D_MODEL
CONV_WIDTH = 31

D_FF = 128 * ((8 * D_MODEL // 3 + 127) // 128)
FFN_CONV_WIDTH = 3

DN_ALPHA = (2.0 * DEPTH) ** 0.25
DN_BETA = (8.0 * DEPTH) ** -0.25
NORM_EPS = 1e-6

IN_SPLITS = (GLA_KEY, GLA_KEY, GLA_VAL, GLA_VAL, GLA_GATE_RANK, GLA_GATE_RANK,
             MLA_Q_RANK, MLA_KV_RANK, MLA_ROPE, 2 * CONV_CH, 3 * D_MODEL)
N_IN = sum(IN_SPLITS)

kernel_name = 'hybrid_gla_mla_conformer_dit_trunk'


def _layer_norm(x, g, b):
    xf = x.astype(jnp.float32)
    mu = jnp.mean(xf, axis=-1, keepdims=True)
    var = jnp.mean(jnp.square(xf - mu), axis=-1, keepdims=True)
    return ((xf - mu) * lax.rsqrt(var + NORM_EPS) * g + b).astype(x.dtype)


def _rms_norm(x, g):
    xf = x.astype(jnp.float32)
    y = xf * lax.rsqrt(jnp.mean(xf * xf, axis=-1, keepdims=True) + NORM_EPS)
    return (y * g).astype(x.dtype)


def _split_cols(u):
    idx = []
    s = 0
    for n in IN_SPLITS[:-1]:
        s += n
        idx.append(s)
    return jnp.split(u, idx, axis=-1)


def _heads(t, n_heads):
    b, t_len, w = t.shape
    return t.reshape(b, t_len, n_heads, w // n_heads).transpose(0, 2, 1, 3)


def _merge_heads(t):
    b, h, t_len, d = t.shape
    return t.transpose(0, 2, 1, 3).reshape(b, t_len, h * d)


def _depthwise_conv(x, w, b):
    k, ch = w.shape
    pad = (k - 1) // 2
    y = lax.conv_general_dilated(x, w[:, None, :], window_strides=(1,), padding=[(pad, pad)],
                                 dimension_numbers=('NWC', 'WIO', 'NWC'), feature_group_count=ch)
    return y + b


def _axial_rope_tables(rows, dtype):
    row = jnp.repeat(jnp.arange(rows, dtype=jnp.float32), GRID_W)
    col = jnp.tile(jnp.arange(GRID_W, dtype=jnp.float32), rows)
    inv = ROPE_BASE ** (-2.0 * jnp.arange(ROPE_F, dtype=jnp.float32) / (MLA_ROPE // 2))
    ang = jnp.stack([row[:, None] * inv, col[:, None] * inv], axis=1)
    return jnp.cos(ang).astype(dtype), jnp.sin(ang).astype(dtype)


def _apply_axial_rope(x, cos, sin):
    shp = x.shape
    xr = x.reshape(shp[:-1] + (2, 2, ROPE_F))
    x0, x1 = xr[..., 0, :], xr[..., 1, :]
    out = jnp.stack([x0 * cos - x1 * sin, x0 * sin + x1 * cos], axis=-2)
    return out.reshape(shp)


def _gla_chunk_scan(q, k, v, log_a, state0):
    b, h, t_len, _ = q.shape
    n = t_len // GLA_CHUNK
    mask = jnp.tril(jnp.ones((GLA_CHUNK, GLA_CHUNK), dtype=bool))[:, :, None]

    def to_chunks(t):
        return jnp.moveaxis(t.reshape(b, h, n, GLA_CHUNK, t.shape[-1]), 2, 0)

    def step(state, inp):
        qc, kc, vc, gc = inp
        qf, kf, vf = qc.astype(jnp.float32), kc.astype(jnp.float32), vc.astype(jnp.float32)
        cum = jnp.cumsum(gc.astype(jnp.float32), axis=-2)
        o_inter = jnp.einsum('bhcd,bhde->bhce', qf * jnp.exp(cum), state)
        diff = cum[:, :, :, None, :] - cum[:, :, None, :, :]
        decay = jnp.exp(jnp.where(mask, diff, -jnp.inf))
        att = jnp.einsum('bhid,bhjd,bhijd->bhij', qf, kf, decay)
        o = o_inter + jnp.einsum('bhij,bhje->bhie', att, vf)
        last = cum[:, :, -1:, :]
        new_state = jnp.exp(last[:, :, 0, :])[..., None] * state + jnp.einsum(
            'bhcd,bhce->bhde', kf * jnp.exp(last - cum), vf)
        return new_state, o

    state, o = lax.scan(step, state0, (to_chunks(q), to_chunks(k), to_chunks(v), to_chunks(log_a)))
    o = jnp.moveaxis(o, 0, 2).reshape(b, h, t_len, v.shape[-1])
    return o, state


def _gla_bidirectional(q, k, v, log_f, log_b, s_f0, s_b0):
    o_f, s_f = _gla_chunk_scan(q, k, v, log_f, s_f0)
    flip = lambda t: jnp.flip(t, axis=2)
    o_b, s_b = _gla_chunk_scan(flip(q), flip(k), flip(v), flip(log_b), s_b0)
    return o_f + flip(o_b), s_f, s_b


def _gla_prep(pt, p):
    q = _heads(pt[0], GLA_HEADS) * (GLA_DK ** -0.5)
    k = _heads(pt[1], GLA_HEADS)
    v = _heads(pt[2], GLA_HEADS)
    log_f = _heads(jax.nn.log_sigmoid((pt[4] @ p['gla_wa_f'] + p['gla_ba_f']).astype(jnp.float32)) / GLA_TAU, GLA_HEADS)
    log_b = _heads(jax.nn.log_sigmoid((pt[5] @ p['gla_wa_b'] + p['gla_ba_b']).astype(jnp.float32)) / GLA_TAU, GLA_HEADS)
    return q, k, v, log_f, log_b


def _gla_out(o, r, p):
    of = o.astype(jnp.float32)
    of = of * lax.rsqrt(jnp.mean(of * of, axis=-1, keepdims=True) + NORM_EPS)
    of = of * p['gla_norm_g'].reshape(GLA_HEADS, 1, GLA_DV)
    y = _merge_heads(of).astype(r.dtype) * jax.nn.silu(r)
    return y @ p['gla_wo']


def _gla_mixer(parts, cparts, p, last):
    bsz = cparts[0].shape[0]
    zero = jnp.zeros((bsz, GLA_HEADS, GLA_DK, GLA_DV), jnp.float32)
    o_c, s_f, s_b = _gla_bidirectional(*_gla_prep(cparts, p), zero, zero)
    o, _, _ = _gla_bidirectional(*_gla_prep(parts, p), s_f, s_b)
    y = _gla_out(o, parts[3], p)
    if last:
        return y, None
    return y, _gla_out(o_c, cparts[3], p)


def _attend(qn, qr, kn, kr, v):
    s = (jnp.einsum('bhqd,bhkd->bhqk', qn, kn, preferred_element_type=jnp.float32)
         + jnp.einsum('bhqd,bkd->bhqk', qr, kr, preferred_element_type=jnp.float32)) * MLA_SCALE
    pr = jax.nn.softmax(s, axis=-1)
    return jnp.einsum('bhqk,bhkd->bhqd', pr.astype(v.dtype), v)


def _mla_latent(qn, qr, kn, kr, v, kn_c, kr_c, v_c):
    b, h, t_len, _ = qn.shape
    nb = t_len // Q_BLOCK
    kn_all = jnp.concatenate([kn, kn_c], axis=2)
    kr_all = jnp.concatenate([kr, kr_c], axis=1)
    v_all = jnp.concatenate([v, v_c], axis=2)

    def blocks(t):
        return jnp.moveaxis(t.reshape(b, h, nb, Q_BLOCK, t.shape[-1]), 2, 0)

    o = lax.map(lambda qb: _attend(qb[0], qb[1], kn_all, kr_all, v_all), (blocks(qn), blocks(qr)))
    return jnp.moveaxis(o, 0, 2).reshape(b, h, t_len, MLA_V)


def _mla_queries(pt, p, cos, sin, rope):
    cq = _rms_norm(pt[6], p['mla_q_norm'])
    q = _heads(cq @ p['mla_wuq'], MLA_HEADS)
    qn, qr = q[..., :MLA_NOPE], q[..., MLA_NOPE:]
    if rope:
        qr = _apply_axial_rope(qr, cos, sin)
    return qn, qr


def _mla_keys(pt, p, cos, sin, rope):
    ckv = _rms_norm(pt[7], p['mla_kv_norm'])
    kv = _heads(ckv @ p['mla_wukv'], MLA_HEADS)
    kn, v = kv[..., :MLA_NOPE], kv[..., MLA_NOPE:]
    kr = pt[8]
    if rope:
        kr = _apply_axial_rope(kr, cos, sin)
    return kn, kr, v


def _mla_mixer(parts, cparts, p, cos, sin, last):
    kn_c, kr_c, v_c = _mla_keys(cparts, p, cos, sin, False)
    kn, kr, v = _mla_keys(parts, p, cos, sin, True)
    qn, qr = _mla_queries(parts, p, cos, sin, True)
    y = _merge_heads(_mla_latent(qn, qr, kn, kr, v, kn_c, kr_c, v_c)) @ p['mla_wo']
    if last:
        return y, None
    qn_c, qr_c = _mla_queries(cparts, p, cos, sin, False)
    return y, _merge_heads(_attend(qn_c, qr_c, kn_c, kr_c, v_c)) @ p['mla_wo']


def _conv_module(u, p):
    a, g = jnp.split(u, 2, axis=-1)
    h = a * jax.nn.sigmoid(g)
    h = _depthwise_conv(h, p['conv_dw'], p['conv_db'])
    h = jax.nn.silu(_layer_norm(h, p['conv_ln_g'], p['conv_ln_b']))
    return h @ p['conv_wo']


def _merge_branches(pt, ys, p):
    g_a, g_b, g_c = jnp.split(jax.nn.sigmoid(pt[10]), 3, axis=-1)
    return (g_a * ys[0] + g_b * ys[1] + g_c * ys[2]) @ p['w_out']


def _token_mixer(h, hc, p, cos, sin, last):
    parts = _split_cols(h @ p['w_in'] + p['b_in'])
    cparts = _split_cols(hc @ p['w_in'] + p['b_in'])
    y_a, yc_a = _gla_mixer(parts, cparts, p, last)
    y_b, yc_b = _mla_mixer(parts, cparts, p, cos, sin, last)
    y = _merge_branches(parts, (y_a, y_b, _conv_module(parts[9], p)), p)
    if last:
        return y, None
    yc = _merge_branches(cparts, (yc_a, yc_b, _conv_module(cparts[9], p)), p)
    return y, yc


def _conv_ffn(h, p):
    u = _depthwise_conv(h @ p['ffn_wup'], p['ffn_dw'], p['ffn_db'])
    g, v = jnp.split(u, 2, axis=-1)
    return (jax.nn.silu(g) * v) @ p['ffn_wdown']


def _trunk_layer(x, xc, mod, cmod, p, cos, sin, last):
    shift1, scale1, gate1, shift2, scale2, gate2 = [m[:, None, :] for m in jnp.split(mod, 6, axis=-1)]
    cshift1, cscale1, cgate1, cshift2, cscale2, cgate2 = jnp.split(cmod, 6, axis=-1)
    y, yc = _token_mixer(x * (1 + scale1) + shift1, xc * (1 + cscale1) + cshift1, p, cos, sin, last)
    x = _layer_norm(DN_ALPHA * x + gate1 * y, p['ln1_g'], p['ln1_b'])
    x = _layer_norm(DN_ALPHA * x + gate2 * _conv_ffn(x * (1 + scale2) + shift2, p), p['ln2_g'], p['ln2_b'])
    if last:
        return x, None
    xc = _layer_norm(DN_ALPHA * xc + cgate1 * yc, p['ln1_g'], p['ln1_b'])
    xc = _layer_norm(DN_ALPHA * xc + cgate2 * _conv_ffn(xc * (1 + cscale2) + cshift2, p), p['ln2_g'], p['ln2_b'])
    return x, xc


def setup_inputs(seed: int = 0) -> dict:
    key = jax.random.key(seed)
    ks = jax.random.split(key, 33)

    def nrm(i, shape, scale):
        return jax.random.normal(ks[i], shape, jnp.float32) * scale

    L = DEPTH
    return {
        'x': nrm(0, (BATCH, SEQ, D_MODEL), 1.0),
        'c': nrm(1, (BATCH, D_MODEL), 1.0),
        'ctx': nrm(2, (BATCH, CTX_LEN, D_MODEL), 1.0),
        'c_ctx': nrm(3, (D_MODEL,), 1.0),
        'w_ada': nrm(4, (L, D_MODEL, 6 * D_MODEL), 0.5 * D_MODEL ** -0.5),
        'b_ada': nrm(5, (L, 6 * D_MODEL), 0.02),
        'w_in': nrm(6, (L, D_MODEL, N_IN), D_MODEL ** -0.5),
        'b_in': nrm(7, (L, N_IN), 0.02),
        'gla_wa_f': nrm(8, (L, GLA_GATE_RANK, GLA_KEY), GLA_GATE_RANK ** -0.5),
        'gla_ba_f': nrm(9, (L, GLA_KEY), 0.02),
        'gla_wa_b': nrm(10, (L, GLA_GATE_RANK, GLA_KEY), GLA_GATE_RANK ** -0.5),
        'gla_ba_b': nrm(11, (L, GLA_KEY), 0.02),
        'gla_norm_g': 1.0 + nrm(12, (L, GLA_VAL), 0.02),
        'gla_wo': nrm(13, (L, GLA_VAL, D_MODEL), DN_BETA * GLA_VAL ** -0.5),
        'mla_q_norm': 1.0 + nrm(14, (L, MLA_Q_RANK), 0.02),
        'mla_kv_norm': 1.0 + nrm(15, (L, MLA_KV_RANK), 0.02),
        'mla_wuq': nrm(16, (L, MLA_Q_RANK, MLA_HEADS * (MLA_NOPE + MLA_ROPE)), MLA_Q_RANK ** -0.5),
        'mla_wukv': nrm(17, (L, MLA_KV_RANK, MLA_HEADS * (MLA_NOPE + MLA_V)), MLA_KV_RANK ** -0.5),
        'mla_wo': nrm(18, (L, MLA_HEADS * MLA_V, D_MODEL), DN_BETA * (MLA_HEADS * MLA_V) ** -0.5),
        'conv_dw': nrm(19, (L, CONV_WIDTH, CONV_CH), CONV_WIDTH ** -0.5),
        'conv_db': nrm(20, (L, CONV_CH), 0.02),
        'conv_ln_g': 1.0 + nrm(21, (L, CONV_CH), 0.02),
        'conv_ln_b': nrm(22, (L, CONV_CH), 0.02),
        'conv_wo': nrm(23, (L, CONV_CH, D_MODEL), DN_BETA * CONV_CH ** -0.5),
        'w_out': nrm(24, (L, D_MODEL, D_MODEL), DN_BETA * D_MODEL ** -0.5),
        'ln1_g': 1.0 + nrm(25, (L, D_MODEL), 0.02),
        'ln1_b': nrm(26, (L, D_MODEL), 0.02),
        'ffn_wup': nrm(27, (L, D_MODEL, 2 * D_FF), D_MODEL ** -0.5),
        'ffn_dw': nrm(28, (L, FFN_CONV_WIDTH, 2 * D_FF), FFN_CONV_WIDTH ** -0.5),
        'ffn_db': nrm(29, (L, 2 * D_FF), 0.02),
        'ffn_wdown': nrm(30, (L, D_FF, D_MODEL), DN_BETA * D_FF ** -0.5),
        'ln2_g': 1.0 + nrm(31, (L, D_MODEL), 0.02),
        'ln2_b': nrm(32, (L, D_MODEL), 0.02),
    }


def reference(x, c, ctx, c_ctx, w_ada, b_ada, w_in, b_in, gla_wa_f, gla_ba_f, gla_wa_b, gla_ba_b,
              gla_norm_g, gla_wo, mla_q_norm, mla_kv_norm, mla_wuq, mla_wukv, mla_wo, conv_dw, conv_db,
              conv_ln_g, conv_ln_b, conv_wo, w_out, ln1_g, ln1_b, ffn_wup, ffn_dw, ffn_db, ffn_wdown,
              ln2_g, ln2_b):
    rows = x.shape[1] // GRID_W
    cos, sin = _axial_rope_tables(rows, x.dtype)
    silu_c = jax.nn.silu(c)
    silu_cc = jax.nn.silu(c_ctx)
    xc = ctx
    for l in range(DEPTH):
        p = {
            'w_in': w_in[l], 'b_in': b_in[l],
            'gla_wa_f': gla_wa_f[l], 'gla_ba_f': gla_ba_f[l], 'gla_wa_b': gla_wa_b[l], 'gla_ba_b': gla_ba_b[l],
            'gla_norm_g': gla_norm_g[l], 'gla_wo': gla_wo[l],
            'mla_q_norm': mla_q_norm[l], 'mla_kv_norm': mla_kv_norm[l], 'mla_wuq': mla_wuq[l],
            'mla_wukv': mla_wukv[l], 'mla_wo': mla_wo[l],
            'conv_dw': conv_dw[l], 'conv_db': conv_db[l], 'conv_ln_g': conv_ln_g[l], 'conv_ln_b': conv_ln_b[l],
            'conv_wo': conv_wo[l], 'w_out': w_out[l], 'ln1_g': ln1_g[l], 'ln1_b': ln1_b[l],
            'ffn_wup': ffn_wup[l], 'ffn_dw': ffn_dw[l], 'ffn_db': ffn_db[l], 'ffn_wdown': ffn_wdown[l],
            'ln2_g': ln2_g[l], 'ln2_b': ln2_b[l],
        }
        mod = silu_c @ w_ada[l] + b_ada[l]
        cmod = silu_cc @ w_ada[l] + b_ada[l]
        x, xc = _trunk_layer(x, xc, mod, cmod, p, cos, sin, l == DEPTH - 1)
    return x
```

```python
import numpy as np
import concourse.bass as bass
import concourse.mybir as mybir
from concourse.bass_utils import run_bass_kernel_spmd

F32 = mybir.dt.float32
BF16 = mybir.dt.bfloat16
AF = mybir.ActivationFunctionType
ALU = mybir.AluOpType
AX = mybir.AxisListType

import os
SAME_ENGINE_SYNC = not bool(os.environ.get("NO_SES"))


class Op:
    __slots__ = ("eng", "fn", "dma", "idx", "waits", "flag", "sem", "val", "gidx")

    def __init__(self, eng, fn, dma, idx):
        self.eng = eng
        self.fn = fn
        self.dma = dma
        self.idx = idx
        self.waits = []
        self.flag = False
        self.sem = None
        self.val = None


class Sched:
    ENGS = ("pe", "act", "dve", "pool", "sp")

    def __init__(self, nc, n_dma_sems=40):
        self.nc = nc
        self.q = {e: [] for e in self.ENGS}
        self.tok_w = {}
        self.tok_r = {}
        self.n_dma_sems = n_dma_sems
        self.dma_ops = []
        self.dma_hw = []
        self.dma_sw = []
        self.seen = {e: {s: -1 for s in self.ENGS} for e in self.ENGS}
        self.seen_dma = {e: set() for e in self.ENGS}

    def add(self, eng, fn, reads=(), writes=(), dma=False):
        q = self.q[eng]
        op = Op(eng, fn, dma, len(q))
        deps = []
        for t in reads:
            w = self.tok_w.get(t)
            if w is not None:
                deps.append(w)
        for t in writes:
            w = self.tok_w.get(t)
            if w is not None:
                deps.append(w)
            deps.extend(self.tok_r.get(t, ()))
        if dma:
            lst = self.dma_sw if eng == "pool" else self.dma_hw
            n = len(lst)
            op.gidx = n
            if n >= self.n_dma_sems:
                deps.append(lst[n - self.n_dma_sems])
            lst.append(op)
            self.dma_ops.append(op)
        best = {}
        seen = self.seen[eng]
        seen_dma = self.seen_dma[eng]
        for d in deps:
            if d is op:
                continue
            if d.dma:
                if id(d) in seen_dma:
                    continue
                seen_dma.add(id(d))
                op.waits.append(d)
                d.flag = True
            else:
                if d.eng == eng and (eng == "pe" or not SAME_ENGINE_SYNC):
                    continue
                if d.idx <= seen[d.eng]:
                    continue
                b = best.get(d.eng)
                if b is None or d.idx > b.idx:
                    best[d.eng] = d
        for e, d in best.items():
            seen[e] = d.idx
            op.waits.append(d)
            d.flag = True
        for t in reads:
            self.tok_r.setdefault(t, []).append(op)
        for t in writes:
            self.tok_w[t] = op
            self.tok_r[t] = []
        q.append(op)
        return op

    def wait_all(self, eng, ops):
        q = self.q[eng]
        op = Op(eng, None, False, len(q))
        for d in ops:
            d.flag = True
            op.waits.append(d)
        q.append(op)
        return op

    def emit(self):
        nc = self.nc
        esem = {e: nc.alloc_semaphore("s_" + e) for e in self.ENGS}
        dsem_hw = [nc.alloc_semaphore("dh_%d" % i) for i in range(min(self.n_dma_sems, max(1, len(self.dma_hw))))]
        dsem_sw = [nc.alloc_semaphore("ds_%d" % i) for i in range(min(self.n_dma_sems, max(1, len(self.dma_sw))))]
        for e in self.ENGS:
            c = 0
            for op in self.q[e]:
                if op.dma:
                    continue
                if op.flag:
                    c += 1
                    op.sem = esem[e]
                    op.val = c
        for lst, dsem in ((self.dma_hw, dsem_hw), (self.dma_sw, dsem_sw)):
            for n, op in enumerate(lst):
                op.sem = dsem[n % self.n_dma_sems]
                op.val = 16 * (n // self.n_dma_sems + 1)
        engobj = {"pe": "tensor", "act": "scalar", "dve": "vector", "pool": "gpsimd", "sp": "sync"}
        with nc.Block() as block:
            for e in self.ENGS:
                ops = self.q[e]
                if not ops:
                    continue

                def body(eng, ops=ops):
                    for op in ops:
                        for d in op.waits:
                            eng.wait_ge(d.sem, d.val)
                        if op.fn is None:
                            continue
                        ins = op.fn(eng)
                        if op.dma:
                            ins.then_inc(op.sem, 16)
                        elif op.flag:
                            ins.then_inc(op.sem, 1)

                getattr(block, engobj[e])(body)

    def barrier(self, dummy_dma):
        lasts = []
        for e in self.ENGS:
            for op in reversed(self.q[e]):
                if op.fn is not None and not op.dma:
                    lasts.append(op)
                    break
        pend = self.dma_hw[-self.n_dma_sems:] + self.dma_sw[-self.n_dma_sems:]
        q = self.q["sp"]
        w = Op("sp", None, False, len(q))
        for d in lasts + pend:
            d.flag = True
            w.waits.append(d)
        q.append(w)
        sig = self.add("sp", dummy_dma, dma=True)
        for e in self.ENGS:
            if e == "sp":
                continue
            qe = self.q[e]
            o = Op(e, None, False, len(qe))
            sig.flag = True
            o.waits.append(sig)
            qe.append(o)
        self.tok_w = {}
        self.tok_r = {}
        for e in self.ENGS:
            for s in self.ENGS:
                last = len(self.q[s]) - 1
                self.seen[e][s] = last
        return sig


D = 1024
TC = 256
TL = 4096
T = TC + TL
NT = T // 128
DEPTH = 4
NIN = 8928
NCH = T // 64
DFF = 2816
EPS = 1e-6
ALPHA = (2.0 * DEPTH) ** 0.25
MLA_SCALE = 192.0 ** -0.5
ALU_ALPHA = ALPHA
GROUPS = [(0, 256)] + [(256 + 512 * i, 512) for i in range(8)]
C_Q, C_K, C_V, C_R, C_GF, C_GB, C_CQ, C_CKV, C_KR, C_CA, C_CG, C_GT = (
    0, 512, 1024, 2048, 3072, 3088, 3104, 3488, 3744, 3808, 4832, 5856)


class Arena:
    def __init__(self, nc, words):
        self.t = nc.alloc_sbuf_tensor("arena", [128, words], F32).ap()
        self.words = words
        self.top = 0

    def reset(self, top=0):
        self.top = top

    def alloc(self, free_shape, dtype, parts=128):
        n = 1
        for s in free_shape:
            n *= s
        nw = n if dtype == F32 else (n + 1) // 2
        nw = (nw + 7) // 8 * 8
        if self.top + nw > self.words:
            raise RuntimeError("arena overflow: need %d have %d" % (self.top + nw, self.words))
        ap = self.t[0:parts, self.top:self.top + nw]
        self.top += nw
        if dtype != F32:
            ap = ap.bitcast(dtype)
        ap = ap[:, 0:n]
        if len(free_shape) == 2:
            ap = ap.rearrange("p (a b) -> p a b", a=free_shape[0])
        elif len(free_shape) == 3:
            ap = ap.rearrange("p (a b c) -> p a b c", a=free_shape[0], b=free_shape[1])
        return ap


class Ring:
    def __init__(self, name, aps):
        self.name = name
        self.aps = aps
        self.i = 0

    def next(self):
        k = self.i % len(self.aps)
        self.i += 1
        return self.aps[k], (self.name, k)


class KB:
    def __init__(self, nc, dbg=None):
        self.nc = nc
        self.S = Sched(nc)
        self.A = Arena(nc, 50 * 1024)
        self.ps = [nc.alloc_psum_tensor("psb%d" % i, [128, 512], F32).ap() for i in range(8)]
        self.dbg = dbg or set()
        self.dummy_src = nc.dram_tensor("dummy_a", [1, 16], F32, kind="Internal").ap()
        self.dummy_dst = nc.dram_tensor("dummy_b", [1, 16], F32, kind="Internal").ap()
        self.outs = []

    def dram(self, name, shape, dtype):
        kind = "ExternalOutput" if name in self.dbg else "Internal"
        return self.nc.dram_tensor(name, shape, dtype, kind=kind).ap()

    def barrier(self):
        a, b = self.dummy_dst, self.dummy_src
        self.S.barrier(lambda e: e.dma_start(out=a, in_=b))

    def dma(self, out, in_, reads=(), writes=(), q="sp", slow=False):
        if slow:
            return self.S.add(q, lambda e: e.dma_start(out=out, in_=in_, allow_slow_non_contiguous=True),
                              reads, writes, dma=True)
        return self.S.add(q, lambda e: e.dma_start(out=out, in_=in_), reads, writes, dma=True)

    def mm(self, out, pairs, reads, writes, start=True, stop=True):
        def fn(e):
            n = len(pairs)
            ins = None
            for i, (l, r) in enumerate(pairs):
                ins = e.matmul(out, lhsT=l, rhs=r, start=(start and i == 0), stop=(stop and i == n - 1))
            return ins
        return self.S.add("pe", fn, reads, writes)

    def transpose(self, out, in_, ident, reads, writes):
        return self.S.add("pe", lambda e: e.transpose(out=out, in_=in_, identity=ident), reads, writes)

    def act(self, out, in_, func, reads, writes, bias=None, scale=None, accum=None):
        kw = {}
        if bias is not None:
            kw["bias"] = bias
        if scale is not None:
            kw["scale"] = scale
        if accum is not None:
            kw["accum_out"] = accum
        return self.S.add("act", lambda e: e.activation(out=out, in_=in_, func=func, **kw), reads, writes)

    def tt(self, out, in0, in1, op, reads, writes, eng="dve"):
        return self.S.add(eng, lambda e: e.tensor_tensor(out=out, in0=in0, in1=in1, op=op), reads, writes)

    def ts(self, out, in0, s1, s2, op0, op1, reads, writes, eng="dve", accum=None):
        if accum is not None:
            return self.S.add(eng, lambda e: e.tensor_scalar(out=out, in0=in0, scalar1=s1, scalar2=s2, op0=op0,
                                                             op1=op1, accum_out=accum), reads, writes)
        if op1 is None:
            return self.S.add(eng, lambda e: e.tensor_scalar(out=out, in0=in0, scalar1=s1, scalar2=None, op0=op0),
                              reads, writes)
        return self.S.add(eng, lambda e: e.tensor_scalar(out=out, in0=in0, scalar1=s1, scalar2=s2, op0=op0, op1=op1),
                          reads, writes)

    def stt(self, out, in0, scalar, in1, op0, op1, reads, writes, accum=None):
        if accum is not None:
            return self.S.add("dve", lambda e: e.scalar_tensor_tensor(out=out, in0=in0, scalar=scalar, in1=in1,
                                                                      op0=op0, op1=op1, accum_out=accum), reads, writes)
        return self.S.add("dve", lambda e: e.scalar_tensor_tensor(out=out, in0=in0, scalar=scalar, in1=in1,
                                                                  op0=op0, op1=op1), reads, writes)

    def copy(self, out, in_, reads, writes, eng="dve"):
        if eng == "act":
            return self.S.add("act", lambda e: e.copy(out=out, in_=in_), reads, writes)
        return self.S.add(eng, lambda e: e.tensor_copy(out=out, in_=in_), reads, writes)

    def memset(self, ap, val, writes, eng="pool"):
        return self.S.add(eng, lambda e: e.memset(ap, val), (), writes)

    def recip(self, out, in_, reads, writes):
        return self.S.add("dve", lambda e: e.reciprocal(out=out, in_=in_), reads, writes)


W_NAMES = ["w_ada", "b_ada", "w_in", "b_in", "gla_wa_f", "gla_ba_f", "gla_wa_b", "gla_ba_b", "gla_norm_g", "gla_wo",
           "mla_q_norm", "mla_kv_norm", "mla_wuq", "mla_wukv", "mla_wo", "conv_dw", "conv_db", "conv_ln_g",
           "conv_ln_b", "conv_wo", "w_out", "ln1_g", "ln1_b", "ffn_wup", "ffn_dw", "ffn_db", "ffn_wdown",
           "ln2_g", "ln2_b"]
W_SHAPES = {
    "w_ada": [4, 1024, 6144], "b_ada": [4, 6144], "w_in": [4, 1024, 8928], "b_in": [4, 8928],
    "gla_wa_f": [4, 16, 512], "gla_ba_f": [4, 512], "gla_wa_b": [4, 16, 512], "gla_ba_b": [4, 512],
    "gla_norm_g": [4, 1024], "gla_wo": [4, 1024, 1024], "mla_q_norm": [4, 384], "mla_kv_norm": [4, 256],
    "mla_wuq": [4, 384, 1536], "mla_wukv": [4, 256, 2048], "mla_wo": [4, 1024, 1024], "conv_dw": [4, 31, 1024],
    "conv_db": [4, 1024], "conv_ln_g": [4, 1024], "conv_ln_b": [4, 1024], "conv_wo": [4, 1024, 1024],
    "w_out": [4, 1024, 1024], "ln1_g": [4, 1024], "ln1_b": [4, 1024], "ffn_wup": [4, 1024, 5632],
    "ffn_dw": [4, 3, 5632], "ffn_db": [4, 5632], "ffn_wdown": [4, 2816, 1024], "ln2_g": [4, 1024],
    "ln2_b": [4, 1024]}


class Prog:
    def __init__(self, dbg=None, n_layers=DEPTH, stop_after=None):
        nc = bass.Bass("TRN2", target_bir_lowering=False)
        self.nc = nc
        kb = KB(nc, dbg)
        self.kb = kb
        self.n_layers = n_layers
        self.stop_after = stop_after
        io = {}
        io["xin"] = nc.dram_tensor("xin", [T, D], F32, kind="ExternalInput").ap()
        io["ccT"] = nc.dram_tensor("ccT", [128, 8, 2], F32, kind="ExternalInput").ap()
        io["rope"] = nc.dram_tensor("rope", [2, 64, T], F32, kind="ExternalInput").ap()
        for n in W_NAMES:
            io[n] = nc.dram_tensor(n, W_SHAPES[n], F32, kind="ExternalInput").ap()
        self.io = io
        kb.dummy_src = io["rope"][0, 0:1, 0:16]
        self.out = nc.dram_tensor("out", [TL, D], F32, kind="ExternalOutput").ap()
        d = kb.dram
        self.MOD = d("MOD", [4, 2, 6144], F32)
        self.X = d("X", [T, D], F32)
        self.QT = d("QT", [512, T], F32)
        self.KT = d("KT", [512, T], F32)
        self.V = d("V", [T, D], BF16)
        self.R = d("R", [T, D], F32)
        self.GFT = d("GFT", [16, T], F32)
        self.GBT = d("GBT", [16, T], F32)
        self.CQT = d("CQT", [384, T], F32)
        self.CKVT = d("CKVT", [256, T], F32)
        self.KRT = d("KRT", [64, T], F32)
        self.KRRT = d("KRRT", [64, T], F32)
        self.HC = d("HC", [D, T], F32)
        self.GATES = d("GATES", [3 * D, T], F32)
        self.QP = d("QP", [2, 4, 128, T], BF16)
        self.KP = d("KP", [2, 4, 128, T], BF16)
        self.KTOK = d("KTOK", [2, 4, 64, NCH, 128], BF16)
        self.ET = d("ET", [2, 4, 128, NCH], F32)
        self.OG = d("OG", [T, D], F32)
        self.MT = d("MT", [D, T], F32)
        self.QN = d("QN", [8, 128, T], BF16)
        self.QR = d("QR", [8, 64, T], BF16)
        self.KN = d("KN", [8, 128, T], BF16)
        self.KRo = d("KRo", [64, T], BF16)
        self.VMH = d("VMH", [8, 128, NT, 128], BF16)
        self.ATT = d("ATT", [D, T], BF16)
        self.H2T = d("H2T", [D, T], BF16)
        self.ACTT = d("ACTT", [DFF, T], BF16)
        self.setup_globals()

    def setup_globals(self):
        kb, A = self.kb, self.kb.A
        self.ident = A.alloc([128], BF16)
        self.ones_bf = A.alloc([128], BF16)
        self.ones_f = A.alloc([128], F32)
        ident = self.ident
        kb.memset(ident, 1.0, ["ident"])
        kb.S.add("pool", lambda e: e.affine_select(out=ident, in_=ident, pattern=[[-1, 128]], compare_op=ALU.is_equal,
                                                   fill=0.0, base=0, channel_multiplier=1), ["ident"], ["ident"])
        kb.memset(self.ones_bf, 1.0, ["ones_bf"])
        kb.memset(self.ones_f, 1.0, ["ones_f"])
        self.mla_bias = A.alloc([8], F32)
        self.ident_f = A.alloc([128], F32)
        identf = self.ident_f
        kb.memset(identf, 1.0, ["ident_f"])
        kb.S.add("pool", lambda e: e.affine_select(out=identf, in_=identf, pattern=[[-1, 128]], compare_op=ALU.is_equal,
                                                   fill=0.0, base=0, channel_multiplier=1), ["ident_f"], ["ident_f"])
        self.eps_col = A.alloc([1], F32)
        kb.memset(self.eps_col, EPS, ["eps_col"])
        self.base = A.top

    def phase0(self):
        kb, A, io = self.kb, self.kb.A, self.io
        A.reset(self.base)
        cin = A.alloc([8, 2], F32)
        scT = A.alloc([8, 2], BF16)
        kb.dma(cin, io["ccT"], writes=["cin"])
        kb.act(scT, cin, AF.Silu, ["cin"], ["scT"])
        wring = Ring("wada", [A.alloc([8, 512], BF16) for _ in range(4)])
        bt = A.alloc([6144], F32, parts=2)
        modsb = A.alloc([6144], F32, parts=2)
        for l in range(self.n_layers):
            kb.dma(bt, io["b_ada"][l].partition_broadcast(2), writes=["bt"])
            wv = io["w_ada"][l].rearrange("(kc p) n -> p kc n", p=128)
            for nb in range(12):
                w, wt = wring.next()
                kb.dma(w, wv[:, :, nb * 512:(nb + 1) * 512], writes=[wt], q="pool")
                ps = kb.ps[nb % 2][0:2, :]
                pt = ("ps", nb % 2)
                kb.mm(ps, [(scT[:, kc, :], w[:, kc, :]) for kc in range(8)], ["scT", wt], [pt])
                kb.tt(modsb[:, nb * 512:(nb + 1) * 512], ps, bt[:, nb * 512:(nb + 1) * 512], ALU.add,
                      [pt, "bt"], ["modsb"])
            for c0 in (1024, 4096):
                kb.ts(modsb[:, c0:c0 + 1024], modsb[:, c0:c0 + 1024], 1.0, None, ALU.add, None, ["modsb"], ["modsb"])
            kb.dma(self.MOD[l], modsb, ["modsb"], [("MOD", l)])
        kb.barrier()

    def load_bc(self, dst, src1d, tok):
        self.kb.dma(dst, src1d.partition_broadcast(128), writes=[tok])

    def phase1(self, l):
        kb, A, io = self.kb, self.kb.A, self.io
        A.reset(self.base)
        X = io["xin"] if l == 0 else self.X
        ident = self.ident
        hT = A.alloc([8, T], BF16)
        mark = A.top
        bcs = {}
        for nm, row, c0 in (("sh_lat", 0, 0), ("sc_lat", 0, 1024), ("sh_ctx", 1, 0), ("sc_ctx", 1, 1024)):
            bcs[nm] = A.alloc([1024], F32)
            self.load_bc(bcs[nm], self.MOD[l, row, c0:c0 + 1024], "bc_" + nm)
        xring = Ring("xt", [A.alloc([1024], F32) for _ in range(3)])
        tring = Ring("t1", [A.alloc([1024], F32) for _ in range(2)])
        hring = Ring("hb", [A.alloc([1024], BF16) for _ in range(2)])
        for i in range(NT):
            sfx = "ctx" if i < 2 else "lat"
            xt, xtok = xring.next()
            kb.dma(xt, X[i * 128:(i + 1) * 128, :], writes=[xtok])
            t1, ttok = tring.next()
            kb.tt(t1, xt, bcs["sc_" + sfx], ALU.mult, [xtok, "bc_sc_" + sfx], [ttok])
            hb, htok = hring.next()
            kb.tt(hb, t1, bcs["sh_" + sfx], ALU.add, [ttok, "bc_sh_" + sfx], [htok], eng="pool")
            pst = kb.ps[6 + (i % 2)].bitcast(BF16)
            ptok = ("ps", 6 + (i % 2))
            for kc in range(8):
                kb.transpose(pst[:, kc * 128:(kc + 1) * 128], hb[:, kc * 128:(kc + 1) * 128], ident,
                             [htok, "ident"], [ptok])
            kb.copy(hT[:, :, i * 128:(i + 1) * 128], pst.rearrange("p (a b) -> p a b", a=8), [ptok], ["hT"], eng="act")
        kb.barrier()
        A.reset(mark)
        bl = io["b_in"][l]
        wv = io["w_in"][l].rearrange("(kc p) n -> p kc n", p=128)
        bias = A.alloc([64], F32)
        kb.memset(bias, 0.0, ["bias"])

        def bias_cols(col, c0, n):
            kb.dma(bias[:, col:col + n], bl[c0:c0 + n * 128].rearrange("(j p) -> p j", p=128), reads=["bias"],
                   writes=["bias"], slow=True)

        def bias_rows(col, c0, nrows, p0=0):
            kb.dma(bias[p0:p0 + nrows, col:col + 1], bl[c0:c0 + nrows].rearrange("(p o) -> p o", o=1),
                   reads=["bias"], writes=["bias"])
        bias_cols(0, C_Q, 4)
        bias_cols(4, C_K, 4)
        bias_cols(8, C_CQ, 3)
        bias_cols(11, C_CKV, 2)
        bias_cols(13, C_CA, 8)
        bias_cols(21, C_CG, 8)
        bias_cols(29, C_GT, 24)
        bias_rows(53, C_GF, 16)
        bias_rows(54, C_GB, 16)
        bias_rows(55, C_KR, 64)
        for a in range(2):
            bias_rows(56, C_KR + a * 32 + 16, 16, p0=a * 32)
            bias_rows(56, C_KR + a * 32, 16, p0=a * 32 + 16)
        bvr = A.alloc([2048], F32)
        self.load_bc(bvr, bl[C_V:C_V + 2048], "bvr")
        wring = Ring("w", [A.alloc([8, 512], BF16) for _ in range(4)])
        fstage = Ring("fst", [A.alloc([T], F32) for _ in range(2)])
        sgring = Ring("sg", [A.alloc([512], F32) for _ in range(2)])
        tstage_f = Ring("tsf", [A.alloc([512], F32) for _ in range(3)])
        tstage_b = Ring("tsb", [A.alloc([512], BF16) for _ in range(3)])
        psring = Ring("ps", [kb.ps[i] for i in range(6)])

        def loadw(c0, n):
            w, wt = wring.next()
            kb.dma(w[:, :, 0:n], wv[:, :, c0:c0 + n], writes=[wt], q="pool")
            return w, wt

        def fm_job(w, wt, off, nrows, bcol, dst, func):
            st, stok = fstage.next()
            for (t0, G) in GROUPS:
                ps, ptok = psring.next()
                kb.mm(ps[0:nrows, 0:G], [(w[:, kc, off:off + nrows], hT[:, kc, t0:t0 + G]) for kc in range(8)],
                      [wt, "hT"], [ptok])
                kb.act(st[0:nrows, t0:t0 + G], ps[0:nrows, 0:G], func, [ptok, "bias"], [stok],
                       bias=bias[0:nrows, bcol:bcol + 1])
            kb.dma(dst, st[0:nrows, :], [stok], [])

        for (c0, dst, b0) in ((C_Q, self.QT, 0), (C_K, self.KT, 4)):
            w, wt = loadw(c0, 512)
            for j in range(4):
                fm_job(w, wt, j * 128, 128, b0 + j, dst[j * 128:(j + 1) * 128, :], AF.Identity)
        w, wt = loadw(C_GF, 416)
        fm_job(w, wt, 0, 16, 53, self.GFT, AF.Identity)
        fm_job(w, wt, 16, 16, 54, self.GBT, AF.Identity)
        for j in range(3):
            fm_job(w, wt, 32 + j * 128, 128, 8 + j, self.CQT[j * 128:(j + 1) * 128, :], AF.Identity)
        w, wt = loadw(C_CKV, 320)
        for a in range(2):
            kb.dma(w[:, :, 320 + a * 32:320 + a * 32 + 16], wv[:, :, C_KR + a * 32 + 16:C_KR + a * 32 + 32],
                   reads=[wt], writes=[wt], q="pool")
            kb.dma(w[:, :, 320 + a * 32 + 16:320 + a * 32 + 32], wv[:, :, C_KR + a * 32:C_KR + a * 32 + 16],
                   reads=[wt], writes=[wt], q="pool")
        for j in range(2):
            fm_job(w, wt, j * 128, 128, 11 + j, self.CKVT[j * 128:(j + 1) * 128, :], AF.Identity)
        fm_job(w, wt, 256, 64, 55, self.KRT, AF.Identity)
        fm_job(w, wt, 320, 64, 56, self.KRRT, AF.Identity)
        for blk in range(2):
            wa, wat = loadw(C_CA + blk * 512, 512)
            wg, wgt = loadw(C_CG + blk * 512, 512)
            for j in range(4):
                ch = blk * 4 + j
                st, stok = fstage.next()
                for (t0, G) in GROUPS:
                    psa, pta = psring.next()
                    psg, ptg = psring.next()
                    kb.mm(psa[:, 0:G], [(wa[:, kc, j * 128:(j + 1) * 128], hT[:, kc, t0:t0 + G]) for kc in range(8)],
                          [wat, "hT"], [pta])
                    kb.mm(psg[:, 0:G], [(wg[:, kc, j * 128:(j + 1) * 128], hT[:, kc, t0:t0 + G]) for kc in range(8)],
                          [wgt, "hT"], [ptg])
                    sg, sgt = sgring.next()
                    kb.act(sg[:, 0:G], psg[:, 0:G], AF.Sigmoid, [ptg, "bias"], [sgt], bias=bias[:, 21 + ch:22 + ch])
                    kb.stt(st[:, t0:t0 + G], psa[:, 0:G], bias[:, 13 + ch:14 + ch], sg[:, 0:G], ALU.add, ALU.mult,
                           [pta, sgt, "bias"], [stok])
                kb.dma(self.HC[ch * 128:(ch + 1) * 128, :], st, [stok], [])
        for blk in range(6):
            w, wt = loadw(C_GT + blk * 512, 512)
            for j in range(4):
                ch = blk * 4 + j
                fm_job(w, wt, j * 128, 128, 29 + ch, self.GATES[ch * 128:(ch + 1) * 128, :], AF.Sigmoid)
        for blk in range(4):
            w, wt = loadw(C_V + blk * 512, 512)
            isv = blk < 2
            for i in range(NT):
                ps, ptok = psring.next()
                kb.mm(ps, [(hT[:, kc, i * 128:(i + 1) * 128], w[:, kc, :]) for kc in range(8)], [wt, "hT"], [ptok])
                st, stok = (tstage_b if isv else tstage_f).next()
                kb.tt(st, ps, bvr[:, blk * 512:(blk + 1) * 512], ALU.add, [ptok, "bvr"], [stok])
                dst = self.V if isv else self.R
                cc = (blk % 2) * 512
                kb.dma(dst[i * 128:(i + 1) * 128, cc:cc + 512], st, [stok], [])
        kb.barrier()

    def phase2_prep(self, l):
        kb, A, io = self.kb, self.kb.A, self.io
        A.reset(self.base)
        ident = self.ident
        cmask = A.alloc([T], F32)
        kb.memset(cmask, 1.0, ["cmask"])
        kb.memset(cmask.rearrange("p (c t) -> p c t", t=64)[:, :, 0:1], 0.0, ["cmask"])
        gT = [A.alloc([T], F32, parts=16) for _ in range(2)]
        kb.dma(gT[0], self.GFT, writes=["gT0"])
        kb.dma(gT[1], self.GBT, writes=["gT1"])
        wa = [A.alloc([512], F32, parts=16) for _ in range(2)]
        kb.dma(wa[0], io["gla_wa_f"][l], writes=["wa0"])
        kb.dma(wa[1], io["gla_wa_b"][l], writes=["wa1"])
        nba = A.alloc([8], F32)
        kb.dma(nba[:, 0:4], io["gla_ba_f"][l].rearrange("(h p) -> p h", p=128), writes=["nba"], slow=True)
        kb.dma(nba[:, 4:8], io["gla_ba_b"][l].rearrange("(h p) -> p h", p=128), reads=["nba"], writes=["nba"], slow=True)
        kb.ts(nba, nba, -1.0, None, ALU.mult, None, ["nba"], ["nba"])
        qT = A.alloc([T], F32)
        kT = A.alloc([T], F32)
        sp = A.alloc([T], F32)
        Pc = A.alloc([T], F32)
        ex = A.alloc([T], F32)
        ering = Ring("e", [A.alloc([512], F32) for _ in range(2)])
        qp = A.alloc([T], BF16)
        kp = A.alloc([T], BF16)
        kpp = A.alloc([T], BF16)
        ktok = A.alloc([NCH, 128], BF16, parts=64)
        et = A.alloc([NCH], F32)
        s_q = 128.0 ** -0.5
        for h in range(4):
            kb.dma(qT, self.QT[h * 128:(h + 1) * 128, :], writes=["qT"])
            kb.dma(kT, self.KT[h * 128:(h + 1) * 128, :], writes=["kT"])
            for d in range(2):
                for gi, (t0, G) in enumerate(GROUPS):
                    ps = kb.ps[gi % 2]
                    pt = ("ps", gi % 2)
                    kb.mm(ps[:, 0:G], [(wa[d][:, h * 128:(h + 1) * 128], gT[d][:, t0:t0 + G])], ["wa%d" % d, "gT%d" % d], [pt])
                    e, etok = ering.next()
                    kb.act(e[:, 0:G], ps[:, 0:G], AF.Exp, [pt, "nba"], [etok], bias=nba[:, d * 4 + h:d * 4 + h + 1], scale=-1.0)
                    kb.act(sp[:, t0:t0 + G], e[:, 0:G], AF.Ln, [etok], ["sp"], bias=1.0)
                kb.S.add("dve", lambda e_, o=Pc, a=cmask, b=sp: e_.tensor_tensor_scan(
                    out=o, data0=a, data1=b, initial=0.0, op0=ALU.mult, op1=ALU.add), ["cmask", "sp"], ["Pc"])
                Pv = Pc.rearrange("p (c t) -> p c t", t=64)
                if d == 0:
                    kb.act(ex, Pc, AF.Exp, ["Pc"], ["ex"], scale=1.0 / 16)
                    kb.tt(kp, kT, ex, ALU.mult, ["kT", "ex"], ["kp"])
                    kb.act(et, Pv[:, :, 63], AF.Exp, ["Pc"], ["et"], scale=-1.0 / 16)
                    kb.tt(ex.rearrange("p (c t) -> p c t", t=64), ex.rearrange("p (c t) -> p c t", t=64),
                          et.unsqueeze(2).to_broadcast([128, NCH, 64]), ALU.mult, ["ex", "et"], ["ex"], eng="pool")
                    kb.tt(kpp, kT, ex, ALU.mult, ["kT", "ex"], ["kpp"])
                    kb.act(ex, Pc, AF.Exp, ["Pc", "kpp"], ["ex"], scale=-1.0 / 16)
                    kb.stt(qp, qT, s_q, ex, ALU.mult, ALU.mult, ["qT", "ex"], ["qp"])
                    ksrc, ksrc_tok = kpp, "kpp"
                else:
                    kb.act(et, Pv[:, :, 63], AF.Exp, ["Pc"], ["et"], scale=-1.0 / 16)
                    kb.tt(sp, Pc, sp, ALU.subtract, ["Pc", "sp"], ["sp"], eng="pool")
                    kb.act(ex, sp, AF.Exp, ["sp"], ["ex"], scale=-1.0 / 16)
                    kb.tt(kp, kT, ex, ALU.mult, ["kT", "ex"], ["kp"])
                    kb.act(ex, sp, AF.Exp, ["sp", "kp"], ["ex"], scale=1.0 / 16)
                    kb.stt(qp, qT, s_q, ex, ALU.mult, ALU.mult, ["qT", "ex"], ["qp"])
                    ksrc, ksrc_tok = kp, "kp"
                for c0 in range(0, NCH, 8):
                    n = min(8, NCH - c0)
                    bank = 6 + ((c0 // 8) % 2)
                    pst = kb.ps[bank].bitcast(BF16)
                    ptok = ("ps", bank)
                    for j in range(n):
                        c = c0 + j
                        kb.transpose(pst[0:64, j * 128:(j + 1) * 128], ksrc[:, c * 64:(c + 1) * 64], ident,
                                     [ksrc_tok, "ident"], [ptok])
                    kb.copy(ktok[:, c0:c0 + n, :], pst[0:64, 0:n * 128].rearrange("p (a b) -> p a b", a=n), [ptok], ["ktok"], eng="act")
                kb.dma(self.QP[d, h], qp, ["qp"], [])
                kb.dma(self.KP[d, h], kp, ["kp"], [])
                kb.dma(self.KTOK[d, h], ktok, ["ktok"], [])
                kb.dma(self.ET[d, h], et, ["et"], [])
        kb.barrier()

    def phase2_scan(self, l):
        kb, A, io = self.kb, self.kb.A, self.io
        A.reset(self.base)
        mask2 = A.alloc([128], F32, parts=64)
        kb.memset(mask2, 1.0, ["mask2"])
        mf, mb = mask2[:, 0:64], mask2[:, 64:128]
        kb.S.add("pool", lambda e: e.affine_select(out=mf, in_=mf, pattern=[[1, 64]], compare_op=ALU.is_ge,
                                                   fill=0.0, base=0, channel_multiplier=-1), ["mask2"], ["mask2"])
        kb.S.add("pool", lambda e: e.affine_select(out=mb, in_=mb, pattern=[[-1, 64]], compare_op=ALU.is_ge,
                                                   fill=0.0, base=0, channel_multiplier=1), ["mask2"], ["mask2"])
        qpf = A.alloc([T], BF16); kpf = A.alloc([T], BF16); qpb = A.alloc([T], BF16); kpb = A.alloc([T], BF16)
        ktf = A.alloc([NCH, 128], BF16, parts=64)
        ktb = A.alloc([NCH, 128], BF16, parts=64)
        vh = A.alloc([NCH, 256], BF16, parts=64)
        etf = A.alloc([NCH], F32); etb = A.alloc([NCH], F32)
        SB = A.alloc([NCH, 256], BF16)
        Sst = [A.alloc([256], F32) for _ in range(2)]
        sbf = Ring("sbf", [A.alloc([256], BF16) for _ in range(3)])
        attr = Ring("att", [A.alloc([128], BF16, parts=64) for _ in range(3)])
        ostr = Ring("ost", [A.alloc([8, 256], F32, parts=64) for _ in range(2)])
        OGv = self.OG.rearrange("(c t) e -> t c e", t=64)
        Vv = self.V.rearrange("(c t) e -> t c e", t=64)
        for h in range(4):
            for (ap, src, tok) in ((qpf, self.QP[0, h], "qpf"), (kpf, self.KP[0, h], "kpf"), (qpb, self.QP[1, h], "qpb"),
                                   (kpb, self.KP[1, h], "kpb"), (ktf, self.KTOK[0, h], "ktf"), (ktb, self.KTOK[1, h], "ktb"),
                                   (etf, self.ET[0, h], "etf"), (etb, self.ET[1, h], "etb"),
                                   (vh, Vv[:, :, h * 256:(h + 1) * 256], "vh")):
                kb.dma(ap, src, writes=[tok])
            kb.memset(Sst[0], 0.0, [("S", 0)])
            cur = 0
            order = [3, 2, 1, 0] + list(range(NCH - 1, 3, -1))
            for n, c in enumerate(order):
                S0, S1 = Sst[cur], Sst[1 - cur]
                kb.act(SB[:, c, :], S0, AF.Copy, [("S", cur), "etb"], ["SB"], scale=etb[:, c:c + 1])
                ps = kb.ps[n % 4][:, 0:256]
                pt = ("ps", n % 4)
                kb.mm(ps, [(ktb[:, c, :], vh[:, c, :])], ["ktb", "vh"], [pt])
                kb.stt(S1, S0, etb[:, c:c + 1], ps, ALU.mult, ALU.add, [("S", cur), "etb", pt], [("S", 1 - cur)])
                cur = 1 - cur
            kb.memset(Sst[cur], 0.0, [("S", cur)])
            def issue_att(c):
                cs_ = slice(c * 64, (c + 1) * 64)
                pa_ = kb.ps[4 + (c % 2)][0:64, 0:128]
                pat_ = ("ps", 4 + (c % 2))
                kb.mm(pa_[:, 0:64], [(kpf[:, cs_], qpf[:, cs_])], ["kpf", "qpf"], [pat_])
                kb.mm(pa_[:, 64:128], [(kpb[:, cs_], qpb[:, cs_])], ["kpb", "qpb"], [pat_])
            issue_att(0)
            for c in range(NCH):
                S0, S1 = Sst[cur], Sst[1 - cur]
                cs = slice(c * 64, (c + 1) * 64)
                if c + 1 < NCH:
                    issue_att(c + 1)
                ps = kb.ps[c % 4][:, 0:256]
                pt = ("ps", c % 4)
                kb.mm(ps, [(ktf[:, c, :], vh[:, c, :])], ["ktf", "vh"], [pt])
                sb, sbt = sbf.next()
                kb.copy(sb, S0, [("S", cur)], [sbt], eng="act")
                pa = kb.ps[4 + (c % 2)][0:64, 0:128]
                pat = ("ps", 4 + (c % 2))
                at, att = attr.next()
                kb.tt(at, pa, mask2, ALU.mult, [pat, "mask2"], [att])
                po = kb.ps[6 + (c % 2)][0:64, 0:256]
                pot = ("ps", 6 + (c % 2))
                kb.mm(po, [(qpf[:, cs], sb), (at[:, 0:64], vh[:, c, :]), (qpb[:, cs], SB[:, c, :]), (at[:, 64:128], vh[:, c, :])],
                      ["qpf", sbt, att, "vh", "qpb", "SB"], [pot])
                if c % 8 == 0:
                    ost, ostt = ostr.next()
                kb.copy(ost[:, c % 8, :], po, [pot], [ostt], eng="act")
                if c % 8 == 7 or c == NCH - 1:
                    c0 = c - (c % 8)
                    n = c - c0 + 1
                    kb.dma(OGv[:, c0:c0 + n, h * 256:(h + 1) * 256], ost[:, 0:n, :], [ostt], [])
                kb.stt(S1, S0, etf[:, c:c + 1], ps, ALU.mult, ALU.add, [("S", cur), "etf", pt], [("S", 1 - cur)])
                cur = 1 - cur
        kb.barrier()

    def load_w_bf(self, dst, src2d, tok, q="pool"):
        self.kb.dma(dst, src2d.rearrange("(kc p) n -> p kc n", p=128), writes=[tok], q=q)

    def phase2_out(self, l):
        kb, A, io = self.kb, self.kb.A, self.io
        A.reset(self.base)
        ident = self.ident
        wo = A.alloc([8, D], BF16)
        self.load_w_bf(wo, io["gla_wo"][l], "wo")
        gbc = A.alloc([D], F32)
        self.load_bc(gbc, io["gla_norm_g"][l], "gbc")
        oring = Ring("o", [A.alloc([D], F32) for _ in range(2)])
        rring = Ring("r", [A.alloc([D], F32) for _ in range(2)])
        gsring = Ring("gs", [A.alloc([D], F32) for _ in range(2)])
        junk = A.alloc([256], F32)
        ybring = Ring("yb", [A.alloc([D], BF16) for _ in range(2)])
        ssr = Ring("ssq", [A.alloc([4], F32) for _ in range(2)])
        rsr = Ring("rstd", [A.alloc([4], F32) for _ in range(2)])
        yTr = Ring("yT", [A.alloc([8, 512], BF16) for _ in range(2)])
        gtr = Ring("gt", [A.alloc([512], F32) for _ in range(3)])
        msr = Ring("ms", [A.alloc([512], F32) for _ in range(3)])
        psr = Ring("ps", [kb.ps[i] for i in range(6)])
        for (t0, G) in GROUPS:
            yT, yTt = yTr.next()
            for ti in range(G // 128):
                r0 = t0 + ti * 128
                o, ot = oring.next()
                r, rt = rring.next()
                kb.dma(o, self.OG[r0:r0 + 128, :], writes=[ot])
                kb.dma(r, self.R[r0:r0 + 128, :], writes=[rt])
                ssq, sst = ssr.next()
                for h in range(4):
                    kb.act(junk, o[:, h * 256:(h + 1) * 256], AF.Square, [ot], ["junk", sst], accum=ssq[:, h:h + 1])
                rstd, rst = rsr.next()
                kb.act(rstd, ssq, AF.Sqrt, [sst], [rst], bias=self.eps_col, scale=1.0 / 256)
                kb.recip(rstd, rstd, [rst], [rst])
                gs, gst = gsring.next()
                kb.act(gs, r, AF.Silu, [rt], [gst])
                kb.tt(gs, gs, gbc, ALU.mult, [gst, "gbc"], [gst], eng="pool")
                yb, ybt = ybring.next()
                for h in range(4):
                    hs = slice(h * 256, (h + 1) * 256)
                    kb.stt(yb[:, hs], o[:, hs], rstd[:, h:h + 1], gs[:, hs], ALU.mult, ALU.mult, [ot, rst, gst], [ybt])
                bank = 6 + (ti % 2)
                pst = kb.ps[bank].bitcast(BF16)
                ptok = ("ps", bank)
                for kc in range(8):
                    kb.transpose(pst[:, kc * 128:(kc + 1) * 128], yb[:, kc * 128:(kc + 1) * 128], ident, [ybt, "ident"], [ptok])
                kb.copy(yT[:, :, ti * 128:(ti + 1) * 128], pst.rearrange("p (a b) -> p a b", a=8), [ptok], [yTt], eng="act")
            for j in range(8):
                ps, pt = psr.next()
                kb.mm(ps[:, 0:G], [(wo[:, kc, j * 128:(j + 1) * 128], yT[:, kc, 0:G]) for kc in range(8)], ["wo", yTt], [pt])
                gt, gtt = gtr.next()
                kb.dma(gt[:, 0:G], self.GATES[j * 128:(j + 1) * 128, t0:t0 + G], writes=[gtt])
                ms, mst = msr.next()
                kb.tt(ms[:, 0:G], ps[:, 0:G], gt[:, 0:G], ALU.mult, [pt, gtt], [mst])
                kb.dma(self.MT[j * 128:(j + 1) * 128, t0:t0 + G], ms[:, 0:G], [mst], [])
        kb.barrier()

    def phase3_prep(self, l):
        kb, A, io = self.kb, self.kb.A, self.io
        A.reset(self.base)
        ones_f, ones_bf = self.ones_f, self.ones_bf
        wq = A.alloc([3, 1536], BF16)
        wqr = A.alloc([3, 8, 64], BF16)
        wkv = A.alloc([2, 2048], BF16)
        gq = A.alloc([3], F32)
        gkv = A.alloc([2], F32)
        kb.dma(gq, io["mla_q_norm"][l].rearrange("(k p) -> p k", p=128), writes=["gq"], slow=True)
        kb.dma(gkv, io["mla_kv_norm"][l].rearrange("(k p) -> p k", p=128), writes=["gkv"], slow=True)
        mark = A.top
        wtmp = A.alloc([3, 1536], F32)
        kb.dma(wtmp, io["mla_wuq"][l].rearrange("(k p) n -> p k n", p=128), writes=["wtmp"])
        for k in range(3):
            kb.ts(wq[:, k, :], wtmp[:, k, :], gq[:, k:k + 1], None, ALU.mult, None, ["wtmp", "gq"], ["wq"])
        wtmp2 = A.alloc([2, 2048], F32)
        kb.dma(wtmp2, io["mla_wukv"][l].rearrange("(k p) n -> p k n", p=128), writes=["wtmp2"])
        for k in range(2):
            kb.ts(wkv[:, k, :], wtmp2[:, k, :], gkv[:, k:k + 1], None, ALU.mult, None, ["wtmp2", "gkv"], ["wkv"])
        wq5 = wq.rearrange("p k (h d) -> p k h d", h=8)[:, :, :, 128:192].rearrange(
            "p k h (a f2 f) -> p k h a f2 f", a=2, f2=2)
        wr5 = wqr.rearrange("p k h (a f2 f) -> p k h a f2 f", a=2, f2=2)
        for a in range(2):
            for hf in range(2):
                kb.copy(wr5[:, :, :, a, hf, :], wq5[:, :, :, a, 1 - hf, :], ["wq"], ["wqr"], eng="pool")
        kb.barrier()
        A.reset(mark)
        wkv4 = wkv.rearrange("p k (h d) -> p k h d", h=8)
        mx = A.alloc([32], F32)
        kb.memset(mx, 0.0, ["mx"])
        cqr = Ring("cq", [A.alloc([3, 512], F32) for _ in range(2)])
        ckvr = Ring("ckv", [A.alloc([2, 512], F32) for _ in range(2)])
        tabr = Ring("tab", [A.alloc([4, 512], F32, parts=64) for _ in range(2)])
        sq = A.alloc([3, 512], F32)
        sqk = A.alloc([2, 512], F32)
        cqb = A.alloc([3, 512], BF16)
        ckvb = A.alloc([2, 512], BF16)
        rq = A.alloc([512], F32)
        rkv = A.alloc([512], F32)
        csr = A.alloc([2, 512], F32, parts=64)
        rtok = Ring("rtok", [A.alloc([2], F32) for _ in range(4)])
        st128 = Ring("s128", [A.alloc([512], BF16) for _ in range(3)])
        st64 = Ring("s64", [A.alloc([512], BF16, parts=64) for _ in range(3)])
        sqb = Ring("sqb", [A.alloc([512], BF16) for _ in range(3)])
        t64 = Ring("t64", [A.alloc([512], F32, parts=64) for _ in range(4)])
        vst = Ring("vst", [A.alloc([1024], BF16) for _ in range(2)])
        mtmp = Ring("mtmp", [A.alloc([1], F32) for _ in range(4)])
        psr = Ring("ps", [kb.ps[i] for i in range(8)])

        def sqmax(stage, stok, parts, col, G):
            s2, s2t = sqb.next()
            kb.act(s2[0:parts, 0:G], stage[0:parts, 0:G], AF.Square, [stok], [s2t])
            ps, pt = psr.next()
            kb.mm(ps[:, 0:G], [(ones_bf[0:parts, :], s2[0:parts, 0:G])], [s2t, "ones_bf"], [pt])
            m, mt = mtmp.next()
            kb.S.add("dve", lambda e, o=m, i=ps[:, 0:G]: e.reduce_max(out=o, in_=i, axis=AX.X), [pt], [mt])
            kb.tt(mx[:, col:col + 1], mx[:, col:col + 1], m, ALU.max, [mt, "mx"], ["mx"])

        def rstd_bc(dst, dtok, sqt, sqtok, nk, G, n):
            ps, pt = psr.next()
            kb.mm(ps[:, 0:G], [(ones_f, sqt[:, k, 0:G]) for k in range(nk)], [sqtok, "ones_f"], [pt])
            kb.act(dst[:, 0:G], ps[:, 0:G], AF.Sqrt, [pt], [dtok], bias=self.eps_col, scale=1.0 / n)
            kb.recip(dst[:, 0:G], dst[:, 0:G], [dtok], [dtok])

        VMHv = self.VMH.rearrange("h p kb d -> p kb h d")
        for (t0, G) in GROUPS:
            cq, cqt = cqr.next()
            ckv, ckvt = ckvr.next()
            tab, tabt = tabr.next()
            kb.dma(cq[:, :, 0:G], self.CQT.rearrange("(k p) t -> p k t", p=128)[:, :, t0:t0 + G], writes=[cqt])
            kb.dma(ckv[:, :, 0:G], self.CKVT.rearrange("(k p) t -> p k t", p=128)[:, :, t0:t0 + G], writes=[ckvt])
            kb.dma(tab[:, 0, 0:G], self.KRT[:, t0:t0 + G], writes=[tabt])
            kb.dma(tab[:, 1, 0:G], self.KRRT[:, t0:t0 + G], reads=[tabt], writes=[tabt])
            kb.dma(tab[:, 2:4, 0:G], io["rope"].rearrange("c p t -> p c t")[:, :, t0:t0 + G], reads=[tabt], writes=[tabt])
            kb.act(sq[:, :, 0:G], cq[:, :, 0:G], AF.Square, [cqt], ["sq"])
            kb.act(sqk[:, :, 0:G], ckv[:, :, 0:G], AF.Square, [ckvt], ["sqk"])
            kb.copy(cqb[:, :, 0:G], cq[:, :, 0:G], [cqt], ["cqb"], eng="pool")
            kb.copy(ckvb[:, :, 0:G], ckv[:, :, 0:G], [ckvt], ["ckvb"], eng="pool")
            rstd_bc(rq, "rq", sq, "sq", 3, G, 384.0)
            rstd_bc(rkv, "rkv", sqk, "sqk", 2, G, 256.0)
            kb.tt(csr[:, 0, 0:G], tab[:, 2, 0:G], rq[0:64, 0:G], ALU.mult, [tabt, "rq"], ["csr"], eng="pool")
            kb.tt(csr[:, 1, 0:G], tab[:, 3, 0:G], rq[0:64, 0:G], ALU.mult, [tabt, "rq"], ["csr"], eng="pool")
            ta, tat = t64.next()
            tb, tbt = t64.next()
            kb.tt(ta[:, 0:G], tab[:, 0, 0:G], tab[:, 2, 0:G], ALU.mult, [tabt], [tat], eng="pool")
            kb.tt(tb[:, 0:G], tab[:, 1, 0:G], tab[:, 3, 0:G], ALU.mult, [tabt], [tbt], eng="pool")
            s6, s6t = st64.next()
            kb.tt(s6[:, 0:G], ta[:, 0:G], tb[:, 0:G], ALU.add, [tat, tbt], [s6t], eng="pool")
            kb.dma(self.KRo[:, t0:t0 + G], s6[:, 0:G], [s6t], [])
            sqmax(s6, s6t, 64, 24, G)
            for h in range(8):
                ps, pt = psr.next()
                kb.mm(ps[:, 0:G], [(wq[:, k, h * 192:h * 192 + 128], cqb[:, k, 0:G]) for k in range(3)], ["wq", "cqb"], [pt])
                s1, s1t = st128.next()
                kb.tt(s1[:, 0:G], ps[:, 0:G], rq[:, 0:G], ALU.mult, [pt, "rq"], [s1t])
                kb.dma(self.QN[h, :, t0:t0 + G], s1[:, 0:G], [s1t], [])
                sqmax(s1, s1t, 128, h, G)
                psa, pta = psr.next()
                psb, ptb = psr.next()
                kb.mm(psa[0:64, 0:G], [(wq[:, k, h * 192 + 128:h * 192 + 192], cqb[:, k, 0:G]) for k in range(3)], ["wq", "cqb"], [pta])
                kb.mm(psb[0:64, 0:G], [(wqr[:, k, h, :], cqb[:, k, 0:G]) for k in range(3)], ["wqr", "cqb"], [ptb])
                ta, tat = t64.next()
                tb, tbt = t64.next()
                kb.tt(ta[:, 0:G], psa[0:64, 0:G], csr[:, 0, 0:G], ALU.mult, [pta, "csr"], [tat])
                kb.tt(tb[:, 0:G], psb[0:64, 0:G], csr[:, 1, 0:G], ALU.mult, [ptb, "csr"], [tbt])
                s6, s6t = st64.next()
                kb.tt(s6[:, 0:G], ta[:, 0:G], tb[:, 0:G], ALU.add, [tat, tbt], [s6t], eng="pool")
                kb.dma(self.QR[h, :, t0:t0 + G], s6[:, 0:G], [s6t], [])
                sqmax(s6, s6t, 64, 8 + h, G)
                ps, pt = psr.next()
                kb.mm(ps[:, 0:G], [(wkv4[:, k, h, 0:128], ckvb[:, k, 0:G]) for k in range(2)], ["wkv", "ckvb"], [pt])
                s1, s1t = st128.next()
                kb.tt(s1[:, 0:G], ps[:, 0:G], rkv[:, 0:G], ALU.mult, [pt, "rkv"], [s1t])
                kb.dma(self.KN[h, :, t0:t0 + G], s1[:, 0:G], [s1t], [])
                sqmax(s1, s1t, 128, 16 + h, G)
            for ti in range(G // 128):
                ts_ = slice(ti * 128, (ti + 1) * 128)
                ps, pt = psr.next()
                kb.mm(ps[:, 0:2], [(sqk[:, k, ts_], ones_f[:, 0:2]) for k in range(2)], ["sqk", "ones_f"], [pt])
                rt, rtt = rtok.next()
                kb.act(rt, ps[:, 0:2], AF.Sqrt, [pt], [rtt], bias=self.eps_col, scale=1.0 / 256)
                kb.recip(rt, rt, [rtt], [rtt])
                vs, vst_t = vst.next()
                for half in range(2):
                    ps, pt = psr.next()
                    kb.mm(ps, [(ckvb[:, k, ts_], wkv4[:, k, half * 4:(half + 1) * 4, 128:256]) for k in range(2)], ["wkv", "ckvb"], [pt])
                    kb.act(vs[:, half * 512:(half + 1) * 512], ps, AF.Copy, [pt, rtt], [vst_t], scale=rt[:, 0:1])
                kbi = (t0 + ti * 128) // 128
                kb.dma(VMHv[:, kbi, :, :], vs.rearrange("p (h d) -> p h d", h=8), [vst_t], [])
        mb = self.mla_bias
        tq = A.alloc([8], F32)
        tk = A.alloc([8], F32)
        kb.tt(tq, mx[:, 0:8], mx[:, 8:16], ALU.add, ["mx"], ["tq"])
        kb.ts(tk, mx[:, 16:24], mx[:, 24:25], None, ALU.add, None, ["mx"], ["tk"])
        kb.tt(tq, tq, tk, ALU.mult, ["tq", "tk"], ["tq"])
        kb.act(tq, tq, AF.Sqrt, ["tq"], ["tq"])
        kb.ts(mb, tq, -MLA_SCALE, None, ALU.mult, None, ["tq"], ["mla_bias"])
        kb.barrier()

    def phase3_attn(self, l):
        kb, A, io = self.kb, self.kb.A, self.io
        A.reset(self.base)
        ones_bf = self.ones_bf
        mb = self.mla_bias
        krt = A.alloc([T], BF16, parts=64)
        kb.dma(krt, self.KRo, writes=["krt"])
        hbuf = []
        for i in range(2):
            hbuf.append(dict(kn=A.alloc([T], BF16), vm=A.alloc([NT, 128], BF16), qn=A.alloc([T], BF16),
                             qr=A.alloc([T], BF16, parts=64)))
        ptr = Ring("pt", [A.alloc([512], BF16) for _ in range(4)])
        rlr = Ring("rl", [A.alloc([512], F32) for _ in range(2)])
        lar = Ring("lacc", [A.alloc([512], F32) for _ in range(2)])
        osr = Ring("os", [A.alloc([512], BF16) for _ in range(2)])
        sps = Ring("sps", [kb.ps[i] for i in range(4)])
        gi = 0
        for h in range(8):
            hb = hbuf[h % 2]
            tk = "h%d" % (h % 2)
            kb.dma(hb["kn"], self.KN[h], writes=[tk + "kn"])
            kb.dma(hb["vm"], self.VMH[h], writes=[tk + "vm"])
            kb.dma(hb["qn"], self.QN[h], writes=[tk + "qn"])
            kb.dma(hb["qr"], self.QR[h], writes=[tk + "qr"])
            for (t0, G) in GROUPS:
                kbs = [0, 1] if t0 == 0 else list(range(NT))
                po = kb.ps[4 + (gi % 2)]
                pot = ("ps", 4 + (gi % 2))
                pl = kb.ps[6 + (gi % 2)]
                plt = ("ps", 6 + (gi % 2))
                lacc, lat = lar.next()
                gi += 1
                def issue_S(kbi, hb=hb, tk=tk, t0=t0, G=G):
                    ks = slice(kbi * 128, (kbi + 1) * 128)
                    ps, pst = sps.next()
                    kb.mm(ps[:, 0:G], [(hb["kn"][:, ks], hb["qn"][:, t0:t0 + G]), (krt[:, ks], hb["qr"][:, t0:t0 + G])],
                          [tk + "kn", tk + "qn", "krt", tk + "qr"], [pst])
                    return ps, pst
                LOOK = 2
                Sq = [issue_S(kbs[i]) for i in range(min(LOOK, len(kbs)))]
                for n, kbi in enumerate(kbs):
                    ps, pst = Sq.pop(0)
                    pt, ptt = ptr.next()
                    kb.act(pt[:, 0:G], ps[:, 0:G], AF.Exp, [pst, "mla_bias"], [ptt], bias=mb[:, h:h + 1], scale=MLA_SCALE)
                    if n + LOOK < len(kbs):
                        Sq.append(issue_S(kbs[n + LOOK]))
                    first, last = (n == 0), (n == len(kbs) - 1)
                    kb.mm(po[:, 0:G], [(hb["vm"][:, kbi, :], pt[:, 0:G])], [tk + "vm", ptt], [pot], start=first, stop=last)
                    if first:
                        kb.copy(lacc[:, 0:G], pt[:, 0:G], [ptt], [lat])
                    else:
                        kb.tt(lacc[:, 0:G], lacc[:, 0:G], pt[:, 0:G], ALU.add, [ptt, lat], [lat])
                kb.mm(pl[:, 0:G], [(self.ones_f, lacc[:, 0:G])], ["ones_f", lat], [plt])
                rl, rlt = rlr.next()
                kb.recip(rl[:, 0:G], pl[:, 0:G], [plt], [rlt])
                os_, ost = osr.next()
                kb.tt(os_[:, 0:G], po[:, 0:G], rl[:, 0:G], ALU.mult, [pot, rlt], [ost])
                kb.dma(self.ATT[h * 128:(h + 1) * 128, t0:t0 + G], os_[:, 0:G], [ost], [])
        kb.barrier()

    def phase3_out(self, l):
        kb, A, io = self.kb, self.kb.A, self.io
        A.reset(self.base)
        wo = A.alloc([8, D], BF16)
        self.load_w_bf(wo, io["mla_wo"][l], "wo")
        atr = Ring("at", [A.alloc([8, 512], BF16) for _ in range(2)])
        gtr = Ring("gt", [A.alloc([512], F32) for _ in range(3)])
        mtr = Ring("mt", [A.alloc([512], F32) for _ in range(3)])
        tmr = Ring("tm", [A.alloc([512], F32) for _ in range(3)])
        psr = Ring("ps", [kb.ps[i] for i in range(6)])
        ATv = self.ATT.rearrange("(k p) t -> p k t", p=128)
        for (t0, G) in GROUPS:
            at, att = atr.next()
            kb.dma(at[:, :, 0:G], ATv[:, :, t0:t0 + G], writes=[att])
            for j in range(8):
                ps, pt = psr.next()
                kb.mm(ps[:, 0:G], [(wo[:, k, j * 128:(j + 1) * 128], at[:, k, 0:G]) for k in range(8)], ["wo", att], [pt])
                gt, gtt = gtr.next()
                mt, mtt = mtr.next()
                kb.dma(gt[:, 0:G], self.GATES[D + j * 128:D + (j + 1) * 128, t0:t0 + G], writes=[gtt])
                kb.dma(mt[:, 0:G], self.MT[j * 128:(j + 1) * 128, t0:t0 + G], writes=[mtt])
                tm, tmt = tmr.next()
                kb.tt(tm[:, 0:G], ps[:, 0:G], gt[:, 0:G], ALU.mult, [pt, gtt], [tmt])
                kb.tt(mt[:, 0:G], tm[:, 0:G], mt[:, 0:G], ALU.add, [tmt, mtt], [mtt], eng="pool")
                kb.dma(self.MT[j * 128:(j + 1) * 128, t0:t0 + G], mt[:, 0:G], [mtt], [])
        kb.barrier()

    def resid_ln(self, R_, ps_halves, ps_toks, xsrc_rows, gate_bc, gate_tok, g_bc, b_bc, gb_toks):
        kb = self.kb
        x, xt = R_["x"].next()
        kb.dma(x, xsrc_rows, writes=[xt])
        a, at = R_["a"].next()
        for half in range(2):
            hs = slice(half * 512, (half + 1) * 512)
            kb.tt(a[:, hs], ps_halves[half], gate_bc[:, hs], ALU.mult, [ps_toks[half], gate_tok], [at])
        kb.stt(a, x, ALU_ALPHA, a, ALU.mult, ALU.add, [xt, at], [at])
        st, stt_ = R_["st"].next()
        for half in range(2):
            kb.S.add("dve", lambda e, o=st[:, half, :], i=a[:, half * 512:(half + 1) * 512]: e.bn_stats(out=o, in_=i), [at], [stt_])
        mv, mvt = R_["mv"].next()
        kb.S.add("dve", lambda e, o=mv, i=st: e.bn_aggr(out=o, in_=i), [stt_], [mvt])
        rs, rst = R_["rs"].next()
        kb.act(rs, mv[:, 1:2], AF.Sqrt, [mvt], [rst], bias=self.eps_col)
        kb.recip(rs, rs, [rst], [rst])
        kb.ts(a, a, mv[:, 0:1], rs[:, 0:1], ALU.subtract, ALU.mult, [at, mvt, rst], [at])
        kb.tt(a, a, g_bc, ALU.mult, [at, gb_toks[0]], [at], eng="pool")
        kb.tt(x, a, b_bc, ALU.add, [at, gb_toks[1]], [xt])
        return x, xt

    def make_resid_rings(self, A):
        return dict(x=Ring("rx", [A.alloc([D], F32) for _ in range(2)]),
                    a=Ring("ra", [A.alloc([D], F32) for _ in range(2)]),
                    st=Ring("rst", [A.alloc([2, 6], F32) for _ in range(2)]),
                    mv=Ring("rmv", [A.alloc([2], F32) for _ in range(2)]),
                    rs=Ring("rrs", [A.alloc([1], F32) for _ in range(2)]))

    def load_T_small(self, dst, src2d, nrows, tok):
        kb, A = self.kb, self.kb.A
        ncol = src2d.shape[1]
        nch = ncol // 128
        tmp = A.alloc([ncol], F32, parts=nrows)
        kb.dma(tmp, src2d, writes=[tok + "_tmp"])
        for c0 in range(0, nch, 8):
            n = min(8, nch - c0)
            bank = (c0 // 8) % 2
            ps = kb.ps[bank]
            pt = ("ps", bank)
            for j in range(n):
                kb.mm(ps[:, j * nrows:(j + 1) * nrows], [(tmp[:, (c0 + j) * 128:(c0 + j + 1) * 128], self.ident_f[0:nrows, 0:nrows])],
                      [tok + "_tmp", "ident_f"], [pt])
            kb.copy(dst[:, c0:c0 + n, :], ps[:, 0:n * nrows].rearrange("p (a b) -> p a b", a=n), [pt], [tok])

    def phase4(self, l):
        kb, A, io = self.kb, self.kb.A, self.io
        A.reset(self.base)
        ident, ones_f = self.ident, self.ones_f
        Xsrc = io["xin"] if l == 0 else self.X
        wco = A.alloc([8, D], BF16)
        wout = A.alloc([8, D], BF16)
        self.load_w_bf(wco, io["conv_wo"][l], "wco")
        self.load_w_bf(wout, io["w_out"][l], "wout")
        cw = A.alloc([8, 31], F32)
        vec = A.alloc([3, 8], F32)
        for i, nm in enumerate(("conv_db", "conv_ln_g", "conv_ln_b")):
            kb.dma(vec[:, i, :], io[nm][l].rearrange("(k p) -> p k", p=128), reads=["vec"], writes=["vec"], slow=True)
        mark = A.top
        self.load_T_small(cw, io["conv_dw"][l], 31, "cw")
        kb.barrier()
        A.reset(mark)
        bc = {}
        for nm, src, c0 in (("g1_lat", self.MOD[l, 0], 2048), ("g1_ctx", self.MOD[l, 1], 2048),
                            ("sh2_lat", self.MOD[l, 0], 3072), ("sc2_lat", self.MOD[l, 0], 4096),
                            ("sh2_ctx", self.MOD[l, 1], 3072), ("sc2_ctx", self.MOD[l, 1], 4096),
                            ("lng", io["ln1_g"][l], 0), ("lnb", io["ln1_b"][l], 0)):
            bc[nm] = A.alloc([D], F32)
            self.load_bc(bc[nm], src[c0:c0 + D], "bc_" + nm)
        hbuf = A.alloc([8, 542], BF16)
        dgr = Ring("dg", [A.alloc([31, 128], BF16) for _ in range(2)])
        acc = A.alloc([8, 512], F32)
        sqr = Ring("sq", [A.alloc([512], F32) for _ in range(2)])
        mean = A.alloc([512], F32)
        msq = A.alloc([512], F32)
        rstd = A.alloc([512], F32)
        tnr = Ring("tn", [A.alloc([512], F32) for _ in range(2)])
        cn = A.alloc([8, 512], BF16)
        mg = A.alloc([8, 512], BF16)
        gtr = Ring("gt", [A.alloc([512], F32) for _ in range(2)])
        mtr = Ring("mt", [A.alloc([512], F32) for _ in range(2)])
        tmr = Ring("tm", [A.alloc([512], F32) for _ in range(2)])
        RR = self.make_resid_rings(A)
        hbr = Ring("hb2", [A.alloc([D], BF16) for _ in range(2)])
        h2s = Ring("h2s", [A.alloc([8, 512], BF16) for _ in range(2)])
        HCv = self.HC.rearrange("(k p) t -> p k t", p=128)
        H2v = self.H2T.rearrange("(k p) t -> p k t", p=128)
        psr = Ring("ps", [kb.ps[i] for i in range(2, 6)])
        for (t0, G) in GROUPS:
            seg_lo, seg_hi = (0, TC) if t0 < TC else (TC, T)
            lo, hi = max(seg_lo, t0 - 15), min(seg_hi, t0 + G + 15)
            if lo > t0 - 15:
                kb.memset(hbuf[:, :, 0:15], 0.0, ["hbuf"])
            if hi < t0 + G + 15:
                kb.memset(hbuf[:, :, 15 + G:30 + G], 0.0, ["hbuf"])
            kb.dma(hbuf[:, :, lo - (t0 - 15):hi - (t0 - 15)], HCv[:, :, lo:hi], reads=["hbuf"], writes=["hbuf"], q="pool")
            for k in range(8):
                dg, dgt = dgr.next()
                kb.tt(dg, ident.unsqueeze(1).to_broadcast([128, 31, 128]),
                      cw[:, k, :].unsqueeze(2).to_broadcast([128, 31, 128]), ALU.mult, ["ident", "cw"], [dgt])
                ps, pt = psr.next()
                kb.mm(ps[:, 0:G], [(dg[:, j, :], hbuf[:, k, j:j + G]) for j in range(31)], [dgt, "hbuf"], [pt])
                kb.act(acc[:, k, 0:G], ps[:, 0:G], AF.Identity, [pt, "vec"], [("acc", k)], bias=vec[:, 0, k:k + 1])
            p1, p2 = kb.ps[0], kb.ps[1]
            for k in range(8):
                kb.mm(p1[:, 0:G], [(ones_f, acc[:, k, 0:G])], [("acc", k), "ones_f"], [("ps", 0)], start=(k == 0), stop=(k == 7))
                s2, s2t = sqr.next()
                kb.act(s2[:, 0:G], acc[:, k, 0:G], AF.Square, [("acc", k)], [s2t])
                kb.mm(p2[:, 0:G], [(ones_f, s2[:, 0:G])], [s2t, "ones_f"], [("ps", 1)], start=(k == 0), stop=(k == 7))
            kb.act(mean[:, 0:G], p1[:, 0:G], AF.Copy, [("ps", 0)], ["mean"], scale=1.0 / D)
            kb.act(msq[:, 0:G], p1[:, 0:G], AF.Square, [("ps", 0)], ["msq"], scale=1.0 / D)
            kb.stt(rstd[:, 0:G], p2[:, 0:G], 1.0 / D, msq[:, 0:G], ALU.mult, ALU.subtract, [("ps", 1), "msq"], ["rstd"])
            kb.act(rstd[:, 0:G], rstd[:, 0:G], AF.Sqrt, ["rstd"], ["rstd"], bias=self.eps_col)
            kb.recip(rstd[:, 0:G], rstd[:, 0:G], ["rstd"], ["rstd"])
            for k in range(8):
                tn, tnt = tnr.next()
                kb.tt(tn[:, 0:G], acc[:, k, 0:G], mean[:, 0:G], ALU.subtract, [("acc", k), "mean"], [tnt])
                kb.tt(tn[:, 0:G], tn[:, 0:G], rstd[:, 0:G], ALU.mult, [tnt, "rstd"], [tnt], eng="pool")
                kb.act(cn[:, k, 0:G], tn[:, 0:G], AF.Silu, [tnt, "vec"], ["cn"], bias=vec[:, 2, k:k + 1], scale=vec[:, 1, k:k + 1])
            for j in range(8):
                ps, pt = psr.next()
                kb.mm(ps[:, 0:G], [(wco[:, k, j * 128:(j + 1) * 128], cn[:, k, 0:G]) for k in range(8)], ["wco", "cn"], [pt])
                gt, gtt = gtr.next()
                mt, mtt = mtr.next()
                kb.dma(gt[:, 0:G], self.GATES[2 * D + j * 128:2 * D + (j + 1) * 128, t0:t0 + G], writes=[gtt])
                kb.dma(mt[:, 0:G], self.MT[j * 128:(j + 1) * 128, t0:t0 + G], writes=[mtt])
                tm, tmt = tmr.next()
                kb.tt(tm[:, 0:G], ps[:, 0:G], gt[:, 0:G], ALU.mult, [pt, gtt], [tmt])
                kb.tt(mg[:, j, 0:G], tm[:, 0:G], mt[:, 0:G], ALU.add, [tmt, mtt], ["mg"], eng="pool")
            h2, h2t = h2s.next()
            for ti in range(G // 128):
                r0 = t0 + ti * 128
                sfx = "ctx" if r0 < TC else "lat"
                halves, toks = [], []
                for half in range(2):
                    ps, pt = psr.next()
                    kb.mm(ps, [(mg[:, k, ti * 128:(ti + 1) * 128], wout[:, k, half * 512:(half + 1) * 512]) for k in range(8)],
                          ["mg", "wout"], [pt])
                    halves.append(ps)
                    toks.append(pt)
                xn, xnt = self.resid_ln(RR, halves, toks, Xsrc[r0:r0 + 128, :], bc["g1_" + sfx], "bc_g1_" + sfx,
                                        bc["lng"], bc["lnb"], ("bc_lng", "bc_lnb"))
                kb.dma(self.X[r0:r0 + 128, :], xn, [xnt], [])
                hb, hbt = hbr.next()
                a, at = RR["a"].next()
                kb.tt(a, xn, bc["sc2_" + sfx], ALU.mult, [xnt, "bc_sc2_" + sfx], [at])
                kb.tt(hb, a, bc["sh2_" + sfx], ALU.add, [at, "bc_sh2_" + sfx], [hbt], eng="pool")
                bank = 6 + (ti % 2)
                pst = kb.ps[bank].bitcast(BF16)
                ptok = ("ps", bank)
                for k in range(8):
                    kb.transpose(pst[:, k * 128:(k + 1) * 128], hb[:, k * 128:(k + 1) * 128], ident, [hbt, "ident"], [ptok])
                kb.copy(h2[:, :, ti * 128:(ti + 1) * 128], pst.rearrange("p (a b) -> p a b", a=8), [ptok], [h2t], eng="act")
            kb.dma(H2v[:, :, t0:t0 + G], h2[:, :, 0:G], [h2t], [])
        kb.barrier()

    def phase6_up(self, l):
        kb, A, io = self.kb, self.kb.A, self.io
        A.reset(self.base)
        dw = A.alloc([44, 3], F32)
        db = A.alloc([44], F32)
        kb.dma(db, io["ffn_db"][l].rearrange("(k p) -> p k", p=128), writes=["db"], slow=True)
        mark = A.top
        self.load_T_small(dw, io["ffn_dw"][l], 3, "dw")
        kb.barrier()
        A.reset(mark)
        h2T = A.alloc([8, T], BF16)
        kb.dma(h2T, self.H2T.rearrange("(k p) t -> p k t", p=128), writes=["h2T"])
        W = T + 1
        bufs = [A.alloc([T + 3], F32) for _ in range(2)]
        us = [A.alloc([W], F32) for _ in range(2)]
        ob = A.alloc([W], BF16)
        wring = Ring("w", [A.alloc([8, 512], BF16) for _ in range(4)])
        wv = io["ffn_wup"][l].rearrange("(kc p) n -> p kc n", p=128)
        psr = Ring("ps", [kb.ps[i] for i in range(8)])
        for b_ in bufs:
            kb.memset(b_[:, 0:1], 0.0, ["gv0", "gv1"])
            kb.memset(b_[:, TC + 1:TC + 2], 0.0, ["gv0", "gv1"])
            kb.memset(b_[:, T + 2:T + 3], 0.0, ["gv0", "gv1"])
        for blk in range(6):
            nck = 4 if blk < 5 else 2
            ws = []
            for s in range(2):
                w, wt = wring.next()
                c0 = s * DFF + blk * 512
                kb.dma(w[:, :, 0:nck * 128], wv[:, :, c0:c0 + nck * 128], writes=[wt], q="pool")
                ws.append((w, wt))
            for j in range(nck):
                c = blk * 4 + j
                for s in range(2):
                    w, wt = ws[s]
                    for (t0, G) in GROUPS:
                        ps, pt = psr.next()
                        kb.mm(ps[:, 0:G], [(w[:, k, j * 128:(j + 1) * 128], h2T[:, k, t0:t0 + G]) for k in range(8)], [wt, "h2T"], [pt])
                        col = t0 + 1 if t0 < TC else t0 + 2
                        kb.copy(bufs[s][:, col:col + G], ps[:, 0:G], [pt], ["gv%d" % s], eng="act")
                    ch = s * 22 + c
                    kb.act(us[s], bufs[s][:, 0:W], AF.Identity, ["gv%d" % s, "dw", "db"], ["u%d" % s],
                           bias=db[:, ch:ch + 1], scale=dw[:, ch, 0:1])
                    for tap in (1, 2):
                        kb.stt(us[s], bufs[s][:, tap:tap + W], dw[:, ch, tap:tap + 1], us[s], ALU.mult, ALU.add,
                               ["gv%d" % s, "dw", "u%d" % s], ["u%d" % s])
                kb.act(us[0], us[0], AF.Silu, ["u0"], ["u0"])
                kb.tt(ob, us[0], us[1], ALU.mult, ["u0", "u1"], ["ob"])
                kb.dma(self.ACTT[c * 128:(c + 1) * 128, 0:TC], ob[:, 0:TC], ["ob"], [])
                kb.dma(self.ACTT[c * 128:(c + 1) * 128, TC:T], ob[:, TC + 1:T + 1], ["ob"], [])
        kb.barrier()

    def phase6_down(self, l, last):
        kb, A, io = self.kb, self.kb.A, self.io
        A.reset(self.base)
        wd = A.alloc([22, D], BF16)
        kb.dma(wd, io["ffn_wdown"][l].rearrange("(c p) n -> p c n", p=128), writes=["wd"], q="pool")
        bc = {}
        for nm, src, c0 in (("g2_lat", self.MOD[l, 0], 5120), ("g2_ctx", self.MOD[l, 1], 5120),
                            ("lng", io["ln2_g"][l], 0), ("lnb", io["ln2_b"][l], 0)):
            bc[nm] = A.alloc([D], F32)
            self.load_bc(bc[nm], src[c0:c0 + D], "bc_" + nm)
        atr = Ring("at", [A.alloc([22, 512], BF16) for _ in range(2)])
        RR = self.make_resid_rings(A)
        ACv = self.ACTT.rearrange("(c p) t -> p c t", p=128)
        psr = Ring("ps", [kb.ps[i] for i in range(8)])
        for (t0, G) in GROUPS:
            if last and t0 < TC:
                continue
            at, att = atr.next()
            kb.dma(at[:, :, 0:G], ACv[:, :, t0:t0 + G], writes=[att])
            for ti in range(G // 128):
                r0 = t0 + ti * 128
                sfx = "ctx" if r0 < TC else "lat"
                halves, toks = [], []
                for half in range(2):
                    ps, pt = psr.next()
                    kb.mm(ps, [(at[:, c, ti * 128:(ti + 1) * 128], wd[:, c, half * 512:(half + 1) * 512]) for c in range(22)],
                          [att, "wd"], [pt])
                    halves.append(ps)
                    toks.append(pt)
                xn, xnt = self.resid_ln(RR, halves, toks, self.X[r0:r0 + 128, :], bc["g2_" + sfx], "bc_g2_" + sfx,
                                        bc["lng"], bc["lnb"], ("bc_lng", "bc_lnb"))
                if last:
                    kb.dma(self.out[r0 - TC:r0 - TC + 128, :], xn, [xnt], [])
                else:
                    kb.dma(self.X[r0:r0 + 128, :], xn, [xnt], [])
        kb.barrier()

    def finish(self):
        kb = self.kb
        kb.S.wait_all("sp", kb.S.dma_hw[-kb.S.n_dma_sems:] + kb.S.dma_sw[-kb.S.n_dma_sems:])
        kb.S.emit()


def rope_tables():
    rows = TL // 64
    row = np.repeat(np.arange(rows, dtype=np.float32), 64)
    col = np.tile(np.arange(64, dtype=np.float32), rows)
    inv = (np.float32(10000.0) ** (np.float32(-2.0) * np.arange(16, dtype=np.float32) / np.float32(32))).astype(np.float32)
    ang = np.stack([row[:, None] * inv, col[:, None] * inv], axis=1).astype(np.float32)
    cos = np.cos(ang).astype(np.float32)
    sin = np.sin(ang).astype(np.float32)
    tab = np.zeros((2, 64, T), np.float32)
    tab[0, :, :TC] = 1.0
    for a in range(2):
        for h in range(2):
            r0 = a * 32 + h * 16
            tab[0, r0:r0 + 16, TC:] = cos[:, a, :].T
            tab[1, r0:r0 + 16, TC:] = (-sin[:, a, :].T) if h == 0 else sin[:, a, :].T
    return tab


def make_in_maps(inputs, cores):
    rope = rope_tables()
    wts = {n: np.ascontiguousarray(np.asarray(inputs[n], dtype=np.float32)) for n in W_NAMES}
    x = np.asarray(inputs["x"], dtype=np.float32)
    c = np.asarray(inputs["c"], dtype=np.float32)
    ctx = np.asarray(inputs["ctx"], dtype=np.float32)
    c_ctx = np.asarray(inputs["c_ctx"], dtype=np.float32)
    maps = []
    for b in cores:
        m = dict(wts)
        m["xin"] = np.ascontiguousarray(np.concatenate([ctx[b], x[b]], axis=0))
        cc = np.stack([c[b], c_ctx], axis=0)
        m["ccT"] = np.ascontiguousarray(cc.reshape(2, 8, 128).transpose(2, 1, 0))
        m["rope"] = rope
        maps.append(m)
    return maps


def build_full():
    P = Prog(n_layers=DEPTH)
    P.phase0()
    for l in range(DEPTH):
        P.phase1(l)
        P.phase2_prep(l)
        P.phase2_scan(l)
        P.phase2_out(l)
        P.phase3_prep(l)
        P.phase3_attn(l)
        P.phase3_out(l)
        P.phase4(l)
        P.phase6_up(l)
        P.phase6_down(l, l == DEPTH - 1)
    P.finish()
    return P


def kernel(**inputs):
    n = 8
    P = build_full()
    maps = make_in_maps(inputs, list(range(n)))
    res = run_bass_kernel_spmd(P.nc, maps, core_ids=list(range(n)))
    out = np.stack([np.asarray(res.results[b]["out"], dtype=np.float32) for b in range(n)], axis=0)
    return out
```

```python
import numpy as np
import concourse.bass as bass
import concourse.mybir as mybir
from concourse.bass_utils import run_bass_kernel_spmd

F32 = mybir.dt.float32
BF16 = mybir.dt.bfloat16
AF = mybir.ActivationFunctionType
ALU = mybir.AluOpType
AX = mybir.AxisListType

import os
SAME_ENGINE_SYNC = not bool(os.environ.get("NO_SES"))


class Op:
    __slots__ = ("eng", "fn", "dma", "idx", "waits", "flag", "sem", "val", "gidx")

    def __init__(self, eng, fn, dma, idx):
        self.eng = eng
        self.fn = fn
        self.dma = dma
        self.idx = idx
        self.waits = []
        self.flag = False
        self.sem = None
        self.val = None


class Sched:
    ENGS = ("pe", "act", "dve", "pool", "sp")

    def __init__(self, nc, n_dma_sems=40):
        self.nc = nc
        self.q = {e: [] for e in self.ENGS}
        self.tok_w = {}
        self.tok_r = {}
        self.n_dma_sems = n_dma_sems
        self.dma_ops = []
        self.dma_hw = []
        self.dma_sw = []
        self.seen = {e: {s: -1 for s in self.ENGS} for e in self.ENGS}
        self.seen_dma = {e: set() for e in self.ENGS}

    def add(self, eng, fn, reads=(), writes=(), dma=False):
        q = self.q[eng]
        op = Op(eng, fn, dma, len(q))
        deps = []
        for t in reads:
            w = self.tok_w.get(t)
            if w is not None:
                deps.append(w)
        for t in writes:
            w = self.tok_w.get(t)
            if w is not None:
                deps.append(w)
            deps.extend(self.tok_r.get(t, ()))
        if dma:
            lst = self.dma_sw if eng == "pool" else self.dma_hw
            n = len(lst)
            op.gidx = n
            if n >= self.n_dma_sems:
                deps.append(lst[n - self.n_dma_sems])
            lst.append(op)
            self.dma_ops.append(op)
        best = {}
        seen = self.seen[eng]
        seen_dma = self.seen_dma[eng]
        for d in deps:
            if d is op:
                continue
            if d.dma:
                if id(d) in seen_dma:
                    continue
                seen_dma.add(id(d))
                op.waits.append(d)
                d.flag = True
            else:
                if d.eng == eng and (eng == "pe" or not SAME_ENGINE_SYNC):
                    continue
                if d.idx <= seen[d.eng]:
                    continue
                b = best.get(d.eng)
                if b is None or d.idx > b.idx:
                    best[d.eng] = d
        for e, d in best.items():
            seen[e] = d.idx
            op.waits.append(d)
            d.flag = True
        for t in reads:
            self.tok_r.setdefault(t, []).append(op)
        for t in writes:
            self.tok_w[t] = op
            self.tok_r[t] = []
        q.append(op)
        return op

    def wait_all(self, eng, ops):
        q = self.q[eng]
        op = Op(eng, None, False, len(q))
        for d in ops:
            d.flag = True
            op.waits.append(d)
        q.append(op)
        return op

    def emit(self):
        nc = self.nc
        esem = {e: nc.alloc_semaphore("s_" + e) for e in self.ENGS}
        dsem_hw = [nc.alloc_semaphore("dh_%d" % i) for i in range(min(self.n_dma_sems, max(1, len(self.dma_hw))))]
        dsem_sw = [nc.alloc_semaphore("ds_%d" % i) for i in range(min(self.n_dma_sems, max(1, len(self.dma_sw))))]
        for e in self.ENGS:
            c = 0
            for op in self.q[e]:
                if op.dma:
                    continue
                if op.flag:
                    c += 1
                    op.sem = esem[e]
                    op.val = c
        for lst, dsem in ((self.dma_hw, dsem_hw), (self.dma_sw, dsem_sw)):
            for n, op in enumerate(lst):
                op.sem = dsem[n % self.n_dma_sems]
                op.val = 16 * (n // self.n_dma_sems + 1)
        engobj = {"pe": "tensor", "act": "scalar", "dve": "vector", "pool": "gpsimd", "sp": "sync"}
        with nc.Block() as block:
            for e in self.ENGS:
                ops = self.q[e]
                if not ops:
                    continue

                def body(eng, ops=ops):
                    for op in ops:
                        for d in op.waits:
                            eng.wait_ge(d.sem, d.val)
                        if op.fn is None:
                            continue
                        ins = op.fn(eng)
                        if op.dma:
                            ins.then_inc(op.sem, 16)
                        elif op.flag:
                            ins.then_inc(op.sem, 1)

                getattr(block, engobj[e])(body)

    def barrier(self, dummy_dma):
        lasts = []
        for e in self.ENGS:
            for op in reversed(self.q[e]):
                if op.fn is not None and not op.dma:
                    lasts.append(op)
                    break
        pend = self.dma_hw[-self.n_dma_sems:] + self.dma_sw[-self.n_dma_sems:]
        q = self.q["sp"]
        w = Op("sp", None, False, len(q))
        for d in lasts + pend:
            d.flag = True
            w.waits.append(d)
        q.append(w)
        sig = self.add("sp", dummy_dma, dma=True)
        for e in self.ENGS:
            if e == "sp":
                continue
            qe = self.q[e]
            o = Op(e, None, False, len(qe))
            sig.flag = True
            o.waits.append(sig)
            qe.append(o)
        self.tok_w = {}
        self.tok_r = {}
        for e in self.ENGS:
            for s in self.ENGS:
                last = len(self.q[s]) - 1
                self.seen[e][s] = last
        return sig


D = 1024
TC = 256
TL = 4096
T = TC + TL
NT = T // 128
DEPTH = 4
NIN = 8928
NCH = T // 64
DFF = 2816
EPS = 1e-6
ALPHA = (2.0 * DEPTH) ** 0.25
MLA_SCALE = 192.0 ** -0.5
ALU_ALPHA = ALPHA
GROUPS = [(0, 256)] + [(256 + 512 * i, 512) for i in range(8)]
C_Q, C_K, C_V, C_R, C_GF, C_GB, C_CQ, C_CKV, C_KR, C_CA, C_CG, C_GT = (
    0, 512, 1024, 2048, 3072, 3088, 3104, 3488, 3744, 3808, 4832, 5856)


class Arena:
    def __init__(self, nc, words):
        self.t = nc.alloc_sbuf_tensor("arena", [128, words], F32).ap()
        self.words = words
        self.top = 0

    def reset(self, top=0):
        self.top = top

    def alloc(self, free_shape, dtype, parts=128):
        n = 1
        for s in free_shape:
            n *= s
        nw = n if dtype == F32 else (n + 1) // 2
        nw = (nw + 7) // 8 * 8
        if self.top + nw > self.words:
            raise RuntimeError("arena overflow: need %d have %d" % (self.top + nw, self.words))
        ap = self.t[0:parts, self.top:self.top + nw]
        self.top += nw
        if dtype != F32:
            ap = ap.bitcast(dtype)
        ap = ap[:, 0:n]
        if len(free_shape) == 2:
            ap = ap.rearrange("p (a b) -> p a b", a=free_shape[0])
        elif len(free_shape) == 3:
            ap = ap.rearrange("p (a b c) -> p a b c", a=free_shape[0], b=free_shape[1])
        return ap


class Ring:
    def __init__(self, name, aps):
        self.name = name
        self.aps = aps
        self.i = 0

    def next(self):
        k = self.i % len(self.aps)
        self.i += 1
        return self.aps[k], (self.name, k)


class KB:
    def __init__(self, nc, dbg=None):
        self.nc = nc
        self.S = Sched(nc)
        self.A = Arena(nc, 50 * 1024)
        self.ps = [nc.alloc_psum_tensor("psb%d" % i, [128, 512], F32).ap() for i in range(8)]
        self.dbg = dbg or set()
        self.dummy_src = nc.dram_tensor("dummy_a", [1, 16], F32, kind="Internal").ap()
        self.dummy_dst = nc.dram_tensor("dummy_b", [1, 16], F32, kind="Internal").ap()
        self.outs = []

    def dram(self, name, shape, dtype):
        kind = "ExternalOutput" if name in self.dbg else "Internal"
        return self.nc.dram_tensor(name, shape, dtype, kind=kind).ap()

    def barrier(self):
        a, b = self.dummy_dst, self.dummy_src
        self.S.barrier(lambda e: e.dma_start(out=a, in_=b))

    def dma(self, out, in_, reads=(), writes=(), q="sp", slow=False):
        if slow:
            return self.S.add(q, lambda e: e.dma_start(out=out, in_=in_, allow_slow_non_contiguous=True),
                              reads, writes, dma=True)
        return self.S.add(q, lambda e: e.dma_start(out=out, in_=in_), reads, writes, dma=True)

    def mm(self, out, pairs, reads, writes, start=True, stop=True):
        def fn(e):
            n = len(pairs)
            ins = None
            for i, (l, r) in enumerate(pairs):
                ins = e.matmul(out, lhsT=l, rhs=r, start=(start and i == 0), stop=(stop and i == n - 1))
            return ins
        return self.S.add("pe", fn, reads, writes)

    def transpose(self, out, in_, ident, reads, writes):
        return self.S.add("pe", lambda e: e.transpose(out=out, in_=in_, identity=ident), reads, writes)

    def act(self, out, in_, func, reads, writes, bias=None, scale=None, accum=None):
        kw = {}
        if bias is not None:
            kw["bias"] = bias
        if scale is not None:
            kw["scale"] = scale
        if accum is not None:
            kw["accum_out"] = accum
        return self.S.add("act", lambda e: e.activation(out=out, in_=in_, func=func, **kw), reads, writes)

    def tt(self, out, in0, in1, op, reads, writes, eng="dve"):
        return self.S.add(eng, lambda e: e.tensor_tensor(out=out, in0=in0, in1=in1, op=op), reads, writes)

    def ts(self, out, in0, s1, s2, op0, op1, reads, writes, eng="dve", accum=None):
        if accum is not None:
            return self.S.add(eng, lambda e: e.tensor_scalar(out=out, in0=in0, scalar1=s1, scalar2=s2, op0=op0,
                                                             op1=op1, accum_out=accum), reads, writes)
        if op1 is None:
            return self.S.add(eng, lambda e: e.tensor_scalar(out=out, in0=in0, scalar1=s1, scalar2=None, op0=op0),
                              reads, writes)
        return self.S.add(eng, lambda e: e.tensor_scalar(out=out, in0=in0, scalar1=s1, scalar2=s2, op0=op0, op1=op1),
                          reads, writes)

    def stt(self, out, in0, scalar, in1, op0, op1, reads, writes, accum=None):
        if accum is not None:
            return self.S.add("dve", lambda e: e.scalar_tensor_tensor(out=out, in0=in0, scalar=scalar, in1=in1,
                                                                      op0=op0, op1=op1, accum_out=accum), reads, writes)
        return self.S.add("dve", lambda e: e.scalar_tensor_tensor(out=out, in0=in0, scalar=scalar, in1=in1,
                                                                  op0=op0, op1=op1), reads, writes)

    def copy(self, out, in_, reads, writes, eng="dve"):
        if eng == "act":
            return self.S.add("act", lambda e: e.copy(out=out, in_=in_), reads, writes)
        return self.S.add(eng, lambda e: e.tensor_copy(out=out, in_=in_), reads, writes)

    def memset(self, ap, val, writes, eng="pool"):
        return self.S.add(eng, lambda e: e.memset(ap, val), (), writes)

    def recip(self, out, in_, reads, writes):
        return self.S.add("dve", lambda e: e.reciprocal(out=out, in_=in_), reads, writes)


W_NAMES = ["w_ada", "b_ada", "w_in", "b_in", "gla_wa_f", "gla_ba_f", "gla_wa_b", "gla_ba_b", "gla_norm_g", "gla_wo",
           "mla_q_norm", "mla_kv_norm", "mla_wuq", "mla_wukv", "mla_wo", "conv_dw", "conv_db", "conv_ln_g",
           "conv_ln_b", "conv_wo", "w_out", "ln1_g", "ln1_b", "ffn_wup", "ffn_dw", "ffn_db", "ffn_wdown",
           "ln2_g", "ln2_b"]
W_SHAPES = {
    "w_ada": [4, 1024, 6144], "b_ada": [4, 6144], "w_in": [4, 1024, 8928], "b_in": [4, 8928],
    "gla_wa_f": [4, 16, 512], "gla_ba_f": [4, 512], "gla_wa_b": [4, 16, 512], "gla_ba_b": [4, 512],
    "gla_norm_g": [4, 1024], "gla_wo": [4, 1024, 1024], "mla_q_norm": [4, 384], "mla_kv_norm": [4, 256],
    "mla_wuq": [4, 384, 1536], "mla_wukv": [4, 256, 2048], "mla_wo": [4, 1024, 1024], "conv_dw": [4, 31, 1024],
    "conv_db": [4, 1024], "conv_ln_g": [4, 1024], "conv_ln_b": [4, 1024], "conv_wo": [4, 1024, 1024],
    "w_out": [4, 1024, 1024], "ln1_g": [4, 1024], "ln1_b": [4, 1024], "ffn_wup": [4, 1024, 5632],
    "ffn_dw": [4, 3, 5632], "ffn_db": [4, 5632], "ffn_wdown": [4, 2816, 1024], "ln2_g": [4, 1024],
    "ln2_b": [4, 1024]}


class Prog:
    def __init__(self, dbg=None, n_layers=DEPTH, stop_after=None):
        nc = bass.Bass("TRN2", target_bir_lowering=False)
        self.nc = nc
        kb = KB(nc, dbg)
        self.kb = kb
        self.n_layers = n_layers
        self.stop_after = stop_after
        io = {}
        io["xin"] = nc.dram_tensor("xin", [T, D], F32, kind="ExternalInput").ap()
        io["ccT"] = nc.dram_tensor("ccT", [128, 8, 2], F32, kind="ExternalInput").ap()
        io["rope"] = nc.dram_tensor("rope", [2, 64, T], F32, kind="ExternalInput").ap()
        for n in W_NAMES:
            io[n] = nc.dram_tensor(n, W_SHAPES[n], F32, kind="ExternalInput").ap()
        self.io = io
        kb.dummy_src = io["rope"][0, 0:1, 0:16]
        self.out = nc.dram_tensor("out", [TL, D], F32, kind="ExternalOutput").ap()
        d = kb.dram
        self.MOD = d("MOD", [4, 2, 6144], F32)
        self.X = d("X", [T, D], F32)
        self.QT = d("QT", [512, T], F32)
        self.KT = d("KT", [512, T], F32)
        self.V = d("V", [T, D], BF16)
        self.R = d("R", [T, D], F32)
        self.GFT = d("GFT", [16, T], F32)
        self.GBT = d("GBT", [16, T], F32)
        self.CQT = d("CQT", [384, T], F32)
        self.CKVT = d("CKVT", [256, T], F32)
        self.KRT = d("KRT", [64, T], F32)
        self.KRRT = d("KRRT", [64, T], F32)
        self.HC = d("HC", [D, T], F32)
        self.GATES = d("GATES", [3 * D, T], F32)
        self.QP = d("QP", [2, 4, 128, T], BF16)
        self.KP = d("KP", [2, 4, 128, T], BF16)
        self.KTOK = d("KTOK", [2, 4, 64, NCH, 128], BF16)
        self.ET = d("ET", [2, 4, 128, NCH], F32)
        self.OG = d("OG", [T, D], F32)
        self.MT = d("MT", [D, T], F32)
        self.QN = d("QN", [8, 128, T], BF16)
        self.QR = d("QR", [8, 64, T], BF16)
        self.KN = d("KN", [8, 128, T], BF16)
        self.KRo = d("KRo", [64, T], BF16)
        self.VMH = d("VMH", [8, 128, NT, 128], BF16)
        self.ATT = d("ATT", [D, T], BF16)
        self.H2T = d("H2T", [D, T], BF16)
        self.ACTT = d("ACTT", [DFF, T], BF16)
        self.setup_globals()

    def setup_globals(self):
        kb, A = self.kb, self.kb.A
        self.ident = A.alloc([128], BF16)
        self.ones_bf = A.alloc([128], BF16)
        self.ones_f = A.alloc([128], F32)
        ident = self.ident
        kb.memset(ident, 1.0, ["ident"])
        kb.S.add("pool", lambda e: e.affine_select(out=ident, in_=ident, pattern=[[-1, 128]], compare_op=ALU.is_equal,
                                                   fill=0.0, base=0, channel_multiplier=1), ["ident"], ["ident"])
        kb.memset(self.ones_bf, 1.0, ["ones_bf"])
        kb.memset(self.ones_f, 1.0, ["ones_f"])
        self.mla_bias = A.alloc([8], F32)
        self.ident_f = A.alloc([128], F32)
        identf = self.ident_f
        kb.memset(identf, 1.0, ["ident_f"])
        kb.S.add("pool", lambda e: e.affine_select(out=identf, in_=identf, pattern=[[-1, 128]], compare_op=ALU.is_equal,
                                                   fill=0.0, base=0, channel_multiplier=1), ["ident_f"], ["ident_f"])
        self.eps_col = A.alloc([1], F32)
        kb.memset(self.eps_col, EPS, ["eps_col"])
        self.base = A.top

    def phase0(self):
        kb, A, io = self.kb, self.kb.A, self.io
        A.reset(self.base)
        cin = A.alloc([8, 2], F32)
        scT = A.alloc([8, 2], BF16)
        kb.dma(cin, io["ccT"], writes=["cin"])
        kb.act(scT, cin, AF.Silu, ["cin"], ["scT"])
        wring = Ring("wada", [A.alloc([8, 512], BF16) for _ in range(4)])
        bt = A.alloc([6144], F32, parts=2)
        modsb = A.alloc([6144], F32, parts=2)
        for l in range(self.n_layers):
            kb.dma(bt, io["b_ada"][l].partition_broadcast(2), writes=["bt"])
            wv = io["w_ada"][l].rearrange("(kc p) n -> p kc n", p=128)
            for nb in range(12):
                w, wt = wring.next()
                kb.dma(w, wv[:, :, nb * 512:(nb + 1) * 512], writes=[wt], q="pool")
                ps = kb.ps[nb % 2][0:2, :]
                pt = ("ps", nb % 2)
                kb.mm(ps, [(scT[:, kc, :], w[:, kc, :]) for kc in range(8)], ["scT", wt], [pt])
                kb.tt(modsb[:, nb * 512:(nb + 1) * 512], ps, bt[:, nb * 512:(nb + 1) * 512], ALU.add,
                      [pt, "bt"], ["modsb"])
            for c0 in (1024, 4096):
                kb.ts(modsb[:, c0:c0 + 1024], modsb[:, c0:c0 + 1024], 1.0, None, ALU.add, None, ["modsb"], ["modsb"])
            kb.dma(self.MOD[l], modsb, ["modsb"], [("MOD", l)])
        kb.barrier()

    def load_bc(self, dst, src1d, tok):
        self.kb.dma(dst, src1d.partition_broadcast(128), writes=[tok])

    def phase1(self, l):
        kb, A, io = self.kb, self.kb.A, self.io
        A.reset(self.base)
        X = io["xin"] if l == 0 else self.X
        ident = self.ident
        hT = A.alloc([8, T], BF16)
        mark = A.top
        bcs = {}
        for nm, row, c0 in (("sh_lat", 0, 0), ("sc_lat", 0, 1024), ("sh_ctx", 1, 0), ("sc_ctx", 1, 1024)):
            bcs[nm] = A.alloc([1024], F32)
            self.load_bc(bcs[nm], self.MOD[l, row, c0:c0 + 1024], "bc_" + nm)
        xring = Ring("xt", [A.alloc([1024], F32) for _ in range(3)])
        tring = Ring("t1", [A.alloc([1024], F32) for _ in range(2)])
        hring = Ring("hb", [A.alloc([1024], BF16) for _ in range(2)])
        for i in range(NT):
            sfx = "ctx" if i < 2 else "lat"
            xt, xtok = xring.next()
            kb.dma(xt, X[i * 128:(i + 1) * 128, :], writes=[xtok])
            t1, ttok = tring.next()
            kb.tt(t1, xt, bcs["sc_" + sfx], ALU.mult, [xtok, "bc_sc_" + sfx], [ttok])
            hb, htok = hring.next()
            kb.tt(hb, t1, bcs["sh_" + sfx], ALU.add, [ttok, "bc_sh_" + sfx], [htok], eng="pool")
            pst = kb.ps[6 + (i % 2)].bitcast(BF16)
            ptok = ("ps", 6 + (i % 2))
            for kc in range(8):
                kb.transpose(pst[:, kc * 128:(kc + 1) * 128], hb[:, kc * 128:(kc + 1) * 128], ident,
                             [htok, "ident"], [ptok])
            kb.copy(hT[:, :, i * 128:(i + 1) * 128], pst.rearrange("p (a b) -> p a b", a=8), [ptok], ["hT"], eng="act")
        kb.barrier()
        A.reset(mark)
        bl = io["b_in"][l]
        wv = io["w_in"][l].rearrange("(kc p) n -> p kc n", p=128)
        bias = A.alloc([64], F32)
        kb.memset(bias, 0.0, ["bias"])

        def bias_cols(col, c0, n):
            kb.dma(bias[:, col:col + n], bl[c0:c0 + n * 128].rearrange("(j p) -> p j", p=128), reads=["bias"],
                   writes=["bias"], slow=True)

        def bias_rows(col, c0, nrows, p0=0):
            kb.dma(bias[p0:p0 + nrows, col:col + 1], bl[c0:c0 + nrows].rearrange("(p o) -> p o", o=1),
                   reads=["bias"], writes=["bias"])
        bias_cols(0, C_Q, 4)
        bias_cols(4, C_K, 4)
        bias_cols(8, C_CQ, 3)
        bias_cols(11, C_CKV, 2)
        bias_cols(13, C_CA, 8)
        bias_cols(21, C_CG, 8)
        bias_cols(29, C_GT, 24)
        bias_rows(53, C_GF, 16)
        bias_rows(54, C_GB, 16)
        bias_rows(55, C_KR, 64)
        for a in range(2):
            bias_rows(56, C_KR + a * 32 + 16, 16, p0=a * 32)
            bias_rows(56, C_KR + a * 32, 16, p0=a * 32 + 16)
        bvr = A.alloc([2048], F32)
        self.load_bc(bvr, bl[C_V:C_V + 2048], "bvr")
        wring = Ring("w", [A.alloc([8, 512], BF16) for _ in range(4)])
        fstage = Ring("fst", [A.alloc([T], F32) for _ in range(2)])
        sgring = Ring("sg", [A.alloc([512], F32) for _ in range(2)])
        tstage_f = Ring("tsf", [A.alloc([512], F32) for _ in range(3)])
        tstage_b = Ring("tsb", [A.alloc([512], BF16) for _ in range(3)])
        psring = Ring("ps", [kb.ps[i] for i in range(6)])

        def loadw(c0, n):
            w, wt = wring.next()
            kb.dma(w[:, :, 0:n], wv[:, :, c0:c0 + n], writes=[wt], q="pool")
            return w, wt

        def fm_job(w, wt, off, nrows, bcol, dst, func):
            st, stok = fstage.next()
            for (t0, G) in GROUPS:
                ps, ptok = psring.next()
                kb.mm(ps[0:nrows, 0:G], [(w[:, kc, off:off + nrows], hT[:, kc, t0:t0 + G]) for kc in range(8)],
                      [wt, "hT"], [ptok])
                kb.act(st[0:nrows, t0:t0 + G], ps[0:nrows, 0:G], func, [ptok, "bias"], [stok],
                       bias=bias[0:nrows, bcol:bcol + 1])
            kb.dma(dst, st[0:nrows, :], [stok], [])

        for (c0, dst, b0) in ((C_Q, self.QT, 0), (C_K, self.KT, 4)):
            w, wt = loadw(c0, 512)
            for j in range(4):
                fm_job(w, wt, j * 128, 128, b0 + j, dst[j * 128:(j + 1) * 128, :], AF.Identity)
        w, wt = loadw(C_GF, 416)
        fm_job(w, wt, 0, 16, 53, self.GFT, AF.Identity)
        fm_job(w, wt, 16, 16, 54, self.GBT, AF.Identity)
        for j in range(3):
            fm_job(w, wt, 32 + j * 128, 128, 8 + j, self.CQT[j * 128:(j + 1) * 128, :], AF.Identity)
        w, wt = loadw(C_CKV, 320)
        for a in range(2):
            kb.dma(w[:, :, 320 + a * 32:320 + a * 32 + 16], wv[:, :, C_KR + a * 32 + 16:C_KR + a * 32 + 32],
                   reads=[wt], writes=[wt], q="pool")
            kb.dma(w[:, :, 320 + a * 32 + 16:320 + a * 32 + 32], wv[:, :, C_KR + a * 32:C_KR + a * 32 + 16],
                   reads=[wt], writes=[wt], q="pool")
        for j in range(2):
            fm_job(w, wt, j * 128, 128, 11 + j, self.CKVT[j * 128:(j + 1) * 128, :], AF.Identity)
        fm_job(w, wt, 256, 64, 55, self.KRT, AF.Identity)
        fm_job(w, wt, 320, 64, 56, self.KRRT, AF.Identity)
        for blk in range(2):
            wa, wat = loadw(C_CA + blk * 512, 512)
            wg, wgt = loadw(C_CG + blk * 512, 512)
            for j in range(4):
                ch = blk * 4 + j
                st, stok = fstage.next()
                for (t0, G) in GROUPS:
                    psa, pta = psring.next()
                    psg, ptg = psring.next()
                    kb.mm(psa[:, 0:G], [(wa[:, kc, j * 128:(j + 1) * 128], hT[:, kc, t0:t0 + G]) for kc in range(8)],
                          [wat, "hT"], [pta])
                    kb.mm(psg[:, 0:G], [(wg[:, kc, j * 128:(j + 1) * 128], hT[:, kc, t0:t0 + G]) for kc in range(8)],
                          [wgt, "hT"], [ptg])
                    sg, sgt = sgring.next()
                    kb.act(sg[:, 0:G], psg[:, 0:G], AF.Sigmoid, [ptg, "bias"], [sgt], bias=bias[:, 21 + ch:22 + ch])
                    kb.stt(st[:, t0:t0 + G], psa[:, 0:G], bias[:, 13 + ch:14 + ch], sg[:, 0:G], ALU.add, ALU.mult,
                           [pta, sgt, "bias"], [stok])
                kb.dma(self.HC[ch * 128:(ch + 1) * 128, :], st, [stok], [])
        for blk in range(6):
            w, wt = loadw(C_GT + blk * 512, 512)
            for j in range(4):
                ch = blk * 4 + j
                fm_job(w, wt, j * 128, 128, 29 + ch, self.GATES[ch * 128:(ch + 1) * 128, :], AF.Sigmoid)
        for blk in range(4):
            w, wt = loadw(C_V + blk * 512, 512)
            isv = blk < 2
            for i in range(NT):
                ps, ptok = psring.next()
                kb.mm(ps, [(hT[:, kc, i * 128:(i + 1) * 128], w[:, kc, :]) for kc in range(8)], [wt, "hT"], [ptok])
                st, stok = (tstage_b if isv else tstage_f).next()
                kb.tt(st, ps, bvr[:, blk * 512:(blk + 1) * 512], ALU.add, [ptok, "bvr"], [stok])
                dst = self.V if isv else self.R
                cc = (blk % 2) * 512
                kb.dma(dst[i * 128:(i + 1) * 128, cc:cc + 512], st, [stok], [])
        kb.barrier()

    def phase2_prep(self, l):
        kb, A, io = self.kb, self.kb.A, self.io
        A.reset(self.base)
        ident = self.ident
        cmask = A.alloc([T], F32)
        kb.memset(cmask, 1.0, ["cmask"])
        kb.memset(cmask.rearrange("p (c t) -> p c t", t=64)[:, :, 0:1], 0.0, ["cmask"])
        gT = [A.alloc([T], F32, parts=16) for _ in range(2)]
        kb.dma(gT[0], self.GFT, writes=["gT0"])
        kb.dma(gT[1], self.GBT, writes=["gT1"])
        wa = [A.alloc([512], F32, parts=16) for _ in range(2)]
        kb.dma(wa[0], io["gla_wa_f"][l], writes=["wa0"])
        kb.dma(wa[1], io["gla_wa_b"][l], writes=["wa1"])
        nba = A.alloc([8], F32)
        kb.dma(nba[:, 0:4], io["gla_ba_f"][l].rearrange("(h p) -> p h", p=128), writes=["nba"], slow=True)
        kb.dma(nba[:, 4:8], io["gla_ba_b"][l].rearrange("(h p) -> p h", p=128), reads=["nba"], writes=["nba"], slow=True)
        kb.ts(nba, nba, -1.0, None, ALU.mult, None, ["nba"], ["nba"])
        qT = A.alloc([T], F32)
        kT = A.alloc([T], F32)
        sp = A.alloc([T], F32)
        Pc = A.alloc([T], F32)
        ex = A.alloc([T], F32)
        ering = Ring("e", [A.alloc([512], F32) for _ in range(2)])
        qp = A.alloc([T], BF16)
        kp = A.alloc([T], BF16)
        kpp = A.alloc([T], BF16)
        ktok = A.alloc([NCH, 128], BF16, parts=64)
        et = A.alloc([NCH], F32)
        s_q = 128.0 ** -0.5
        for h in range(4):
            kb.dma(qT, self.QT[h * 128:(h + 1) * 128, :], writes=["qT"])
            kb.dma(kT, self.KT[h * 128:(h + 1) * 128, :], writes=["kT"])
            for d in range(2):
                for gi, (t0, G) in enumerate(GROUPS):
                    ps = kb.ps[gi % 2]
                    pt = ("ps", gi % 2)
                    kb.mm(ps[:, 0:G], [(wa[d][:, h * 128:(h + 1) * 128], gT[d][:, t0:t0 + G])], ["wa%d" % d, "gT%d" % d], [pt])
                    e, etok = ering.next()
                    kb.act(e[:, 0:G], ps[:, 0:G], AF.Exp, [pt, "nba"], [etok], bias=nba[:, d * 4 + h:d * 4 + h + 1], scale=-1.0)
                    kb.act(sp[:, t0:t0 + G], e[:, 0:G], AF.Ln, [etok], ["sp"], bias=1.0)
                kb.S.add("dve", lambda e_, o=Pc, a=cmask, b=sp: e_.tensor_tensor_scan(
                    out=o, data0=a, data1=b, initial=0.0, op0=ALU.mult, op1=ALU.add), ["cmask", "sp"], ["Pc"])
                Pv = Pc.rearrange("p (c t) -> p c t", t=64)
                if d == 0:
                    kb.act(ex, Pc, AF.Exp, ["Pc"], ["ex"], scale=1.0 / 16)
                    kb.tt(kp, kT, ex, ALU.mult, ["kT", "ex"], ["kp"])
                    kb.act(et, Pv[:, :, 63], AF.Exp, ["Pc"], ["et"], scale=-1.0 / 16)
                    kb.tt(ex.rearrange("p (c t) -> p c t", t=64), ex.rearrange("p (c t) -> p c t", t=64),
                          et.unsqueeze(2).to_broadcast([128, NCH, 64]), ALU.mult, ["ex", "et"], ["ex"], eng="pool")
                    kb.tt(kpp, kT, ex, ALU.mult, ["kT", "ex"], ["kpp"])
                    kb.act(ex, Pc, AF.Exp, ["Pc", "kpp"], ["ex"], scale=-1.0 / 16)
                    kb.stt(qp, qT, s_q, ex, ALU.mult, ALU.mult, ["qT", "ex"], ["qp"])
                    ksrc, ksrc_tok = kpp, "kpp"
                else:
                    kb.act(et, Pv[:, :, 63], AF.Exp, ["Pc"], ["et"], scale=-1.0 / 16)
                    kb.tt(sp, Pc, sp, ALU.subtract, ["Pc", "sp"], ["sp"], eng="pool")
                    kb.act(ex, sp, AF.Exp, ["sp"], ["ex"], scale=-1.0 / 16)
                    kb.tt(kp, kT, ex, ALU.mult, ["kT", "ex"], ["kp"])
                    kb.act(ex, sp, AF.Exp, ["sp", "kp"], ["ex"], scale=1.0 / 16)
                    kb.stt(qp, qT, s_q, ex, ALU.mult, ALU.mult, ["qT", "ex"], ["qp"])
                    ksrc, ksrc_tok = kp, "kp"
                for c0 in range(0, NCH, 8):
                    n = min(8, NCH - c0)
                    bank = 6 + ((c0 // 8) % 2)
                    pst = kb.ps[bank].bitcast(BF16)
                    ptok = ("ps", bank)
                    for j in range(n):
                        c = c0 + j
                        kb.transpose(pst[0:64, j * 128:(j + 1) * 128], ksrc[:, c * 64:(c + 1) * 64], ident,
                                     [ksrc_tok, "ident"], [ptok])
                    kb.copy(ktok[:, c0:c0 + n, :], pst[0:64, 0:n * 128].rearrange("p (a b) -> p a b", a=n), [ptok], ["ktok"], eng="act")
                kb.dma(self.QP[d, h], qp, ["qp"], [])
                kb.dma(self.KP[d, h], kp, ["kp"], [])
                kb.dma(self.KTOK[d, h], ktok, ["ktok"], [])
                kb.dma(self.ET[d, h], et, ["et"], [])
        kb.barrier()

    def phase2_scan(self, l):
        kb, A, io = self.kb, self.kb.A, self.io
        A.reset(self.base)
        mask2 = A.alloc([128], F32, parts=64)
        kb.memset(mask2, 1.0, ["mask2"])
        mf, mb = mask2[:, 0:64], mask2[:, 64:128]
        kb.S.add("pool", lambda e: e.affine_select(out=mf, in_=mf, pattern=[[1, 64]], compare_op=ALU.is_ge,
                                                   fill=0.0, base=0, channel_multiplier=-1), ["mask2"], ["mask2"])
        kb.S.add("pool", lambda e: e.affine_select(out=mb, in_=mb, pattern=[[-1, 64]], compare_op=ALU.is_ge,
                                                   fill=0.0, base=0, channel_multiplier=1), ["mask2"], ["mask2"])
        qpf = A.alloc([T], BF16); kpf = A.alloc([T], BF16); qpb = A.alloc([T], BF16); kpb = A.alloc([T], BF16)
        ktf = A.alloc([NCH, 128], BF16, parts=64)
        ktb = A.alloc([NCH, 128], BF16, parts=64)
        vh = A.alloc([NCH, 256], BF16, parts=64)
        etf = A.alloc([NCH], F32); etb = A.alloc([NCH], F32)
        SB = A.alloc([NCH, 256], BF16)
        Sst = [A.alloc([256], F32) for _ in range(2)]
        sbf = Ring("sbf", [A.alloc([256], BF16) for _ in range(3)])
        attr = Ring("att", [A.alloc([128], BF16, parts=64) for _ in range(3)])
        ostr = Ring("ost", [A.alloc([8, 256], F32, parts=64) for _ in range(2)])
        OGv = self.OG.rearrange("(c t) e -> t c e", t=64)
        Vv = self.V.rearrange("(c t) e -> t c e", t=64)
        for h in range(4):
            for (ap, src, tok) in ((qpf, self.QP[0, h], "qpf"), (kpf, self.KP[0, h], "kpf"), (qpb, self.QP[1, h], "qpb"),
                                   (kpb, self.KP[1, h], "kpb"), (ktf, self.KTOK[0, h], "ktf"), (ktb, self.KTOK[1, h], "ktb"),
                                   (etf, self.ET[0, h], "etf"), (etb, self.ET[1, h], "etb"),
                                   (vh, Vv[:, :, h * 256:(h + 1) * 256], "vh")):
                kb.dma(ap, src, writes=[tok])
            kb.memset(Sst[0], 0.0, [("S", 0)])
            cur = 0
            order = [3, 2, 1, 0] + list(range(NCH - 1, 3, -1))
            for n, c in enumerate(order):
                S0, S1 = Sst[cur], Sst[1 - cur]
                kb.act(SB[:, c, :], S0, AF.Copy, [("S", cur), "etb"], ["SB"], scale=etb[:, c:c + 1])
                ps = kb.ps[n % 4][:, 0:256]
                pt = ("ps", n % 4)
                kb.mm(ps, [(ktb[:, c, :], vh[:, c, :])], ["ktb", "vh"], [pt])
                kb.stt(S1, S0, etb[:, c:c + 1], ps, ALU.mult, ALU.add, [("S", cur), "etb", pt], [("S", 1 - cur)])
                cur = 1 - cur
            kb.memset(Sst[cur], 0.0, [("S", cur)])
            def issue_att(c):
                cs_ = slice(c * 64, (c + 1) * 64)
                pa_ = kb.ps[4 + (c % 2)][0:64, 0:128]
                pat_ = ("ps", 4 + (c % 2))
                kb.mm(pa_[:, 0:64], [(kpf[:, cs_], qpf[:, cs_])], ["kpf", "qpf"], [pat_])
                kb.mm(pa_[:, 64:128], [(kpb[:, cs_], qpb[:, cs_])], ["kpb", "qpb"], [pat_])
            issue_att(0)
            for c in range(NCH):
                S0, S1 = Sst[cur], Sst[1 - cur]
                cs = slice(c * 64, (c + 1) * 64)
                if c + 1 < NCH:
                    issue_att(c + 1)
                ps = kb.ps[c % 4][:, 0:256]
                pt = ("ps", c % 4)
                kb.mm(ps, [(ktf[:, c, :], vh[:, c, :])], ["ktf", "vh"], [pt])
                sb, sbt = sbf.next()
                kb.copy(sb, S0, [("S", cur)], [sbt], eng="act")
                pa = kb.ps[4 + (c % 2)][0:64, 0:128]
                pat = ("ps", 4 + (c % 2))
                at, att = attr.next()
                kb.tt(at, pa, mask2, ALU.mult, [pat, "mask2"], [att])
                po = kb.ps[6 + (c % 2)][0:64, 0:256]
                pot = ("ps", 6 + (c % 2))
                kb.mm(po, [(qpf[:, cs], sb), (at[:, 0:64], vh[:, c, :]), (qpb[:, cs], SB[:, c, :]), (at[:, 64:128], vh[:, c, :])],
                      ["qpf", sbt, att, "vh", "qpb", "SB"], [pot])
                if c % 8 == 0:
                    ost, ostt = ostr.next()
                kb.copy(ost[:, c % 8, :], po, [pot], [ostt], eng="act")
                if c % 8 == 7 or c == NCH - 1:
                    c0 = c - (c % 8)
                    n = c - c0 + 1
                    kb.dma(OGv[:, c0:c0 + n, h * 256:(h + 1) * 256], ost[:, 0:n, :], [ostt], [])
                kb.stt(S1, S0, etf[:, c:c + 1], ps, ALU.mult, ALU.add, [("S", cur), "etf", pt], [("S", 1 - cur)])
                cur = 1 - cur
        kb.barrier()

    def load_w_bf(self, dst, src2d, tok, q="pool"):
        self.kb.dma(dst, src2d.rearrange("(kc p) n -> p kc n", p=128), writes=[tok], q=q)

    def phase2_out(self, l):
        kb, A, io = self.kb, self.kb.A, self.io
        A.reset(self.base)
        ident = self.ident
        wo = A.alloc([8, D], BF16)
        self.load_w_bf(wo, io["gla_wo"][l], "wo")
        gbc = A.alloc([D], F32)
        self.load_bc(gbc, io["gla_norm_g"][l], "gbc")
        oring = Ring("o", [A.alloc([D], F32) for _ in range(2)])
        rring = Ring("r", [A.alloc([D], F32) for _ in range(2)])
        gsring = Ring("gs", [A.alloc([D], F32) for _ in range(2)])
        junk = A.alloc([256], F32)
        ybring = Ring("yb", [A.alloc([D], BF16) for _ in range(2)])
        ssr = Ring("ssq", [A.alloc([4], F32) for _ in range(2)])
        rsr = Ring("rstd", [A.alloc([4], F32) for _ in range(2)])
        yTr = Ring("yT", [A.alloc([8, 512], BF16) for _ in range(2)])
        gtr = Ring("gt", [A.alloc([512], F32) for _ in range(3)])
        msr = Ring("ms", [A.alloc([512], F32) for _ in range(3)])
        psr = Ring("ps", [kb.ps[i] for i in range(6)])
        for (t0, G) in GROUPS:
            yT, yTt = yTr.next()
            for ti in range(G // 128):
                r0 = t0 + ti * 128
                o, ot = oring.next()
                r, rt = rring.next()
                kb.dma(o, self.OG[r0:r0 + 128, :], writes=[ot])
                kb.dma(r, self.R[r0:r0 + 128, :], writes=[rt])
                ssq, sst = ssr.next()
                for h in range(4):
                    kb.act(junk, o[:, h * 256:(h + 1) * 256], AF.Square, [ot], ["junk", sst], accum=ssq[:, h:h + 1])
                rstd, rst = rsr.next()
                kb.act(rstd, ssq, AF.Sqrt, [sst], [rst], bias=self.eps_col, scale=1.0 / 256)
                kb.recip(rstd, rstd, [rst], [rst])
                gs, gst = gsring.next()
                kb.act(gs, r, AF.Silu, [rt], [gst])
                kb.tt(gs, gs, gbc, ALU.mult, [gst, "gbc"], [gst], eng="pool")
                yb, ybt = ybring.next()
                for h in range(4):
                    hs = slice(h * 256, (h + 1) * 256)
                    kb.stt(yb[:, hs], o[:, hs], rstd[:, h:h + 1], gs[:, hs], ALU.mult, ALU.mult, [ot, rst, gst], [ybt])
                bank = 6 + (ti % 2)
                pst = kb.ps[bank].bitcast(BF16)
                ptok = ("ps", bank)
                for kc in range(8):
                    kb.transpose(pst[:, kc * 128:(kc + 1) * 128], yb[:, kc * 128:(kc + 1) * 128], ident, [ybt, "ident"], [ptok])
                kb.copy(yT[:, :, ti * 128:(ti + 1) * 128], pst.rearrange("p (a b) -> p a b", a=8), [ptok], [yTt], eng="act")
            for j in range(8):
                ps, pt = psr.next()
                kb.mm(ps[:, 0:G], [(wo[:, kc, j * 128:(j + 1) * 128], yT[:, kc, 0:G]) for kc in range(8)], ["wo", yTt], [pt])
                gt, gtt = gtr.next()
                kb.dma(gt[:, 0:G], self.GATES[j * 128:(j + 1) * 128, t0:t0 + G], writes=[gtt])
                ms, mst = msr.next()
                kb.tt(ms[:, 0:G], ps[:, 0:G], gt[:, 0:G], ALU.mult, [pt, gtt], [mst])
                kb.dma(self.MT[j * 128:(j + 1) * 128, t0:t0 + G], ms[:, 0:G], [mst], [])
        kb.barrier()

    def phase3_prep(self, l):
        kb, A, io = self.kb, self.kb.A, self.io
        A.reset(self.base)
        ones_f, ones_bf = self.ones_f, self.ones_bf
        wq = A.alloc([3, 1536], BF16)
        wqr = A.alloc([3, 8, 64], BF16)
        wkv = A.alloc([2, 2048], BF16)
        gq = A.alloc([3], F32)
        gkv = A.alloc([2], F32)
        kb.dma(gq, io["mla_q_norm"][l].rearrange("(k p) -> p k", p=128), writes=["gq"], slow=True)
        kb.dma(gkv, io["mla_kv_norm"][l].rearrange("(k p) -> p k", p=128), writes=["gkv"], slow=True)
        mark = A.top
        wtmp = A.alloc([3, 1536], F32)
        kb.dma(wtmp, io["mla_wuq"][l].rearrange("(k p) n -> p k n", p=128), writes=["wtmp"])
        for k in range(3):
            kb.ts(wq[:, k, :], wtmp[:, k, :], gq[:, k:k + 1], None, ALU.mult, None, ["wtmp", "gq"], ["wq"])
        wtmp2 = A.alloc([2, 2048], F32)
        kb.dma(wtmp2, io["mla_wukv"][l].rearrange("(k p) n -> p k n", p=128), writes=["wtmp2"])
        for k in range(2):
            kb.ts(wkv[:, k, :], wtmp2[:, k, :], gkv[:, k:k + 1], None, ALU.mult, None, ["wtmp2", "gkv"], ["wkv"])
        wq5 = wq.rearrange("p k (h d) -> p k h d", h=8)[:, :, :, 128:192].rearrange(
            "p k h (a f2 f) -> p k h a f2 f", a=2, f2=2)
        wr5 = wqr.rearrange("p k h (a f2 f) -> p k h a f2 f", a=2, f2=2)
        for a in range(2):
            for hf in range(2):
                kb.copy(wr5[:, :, :, a, hf, :], wq5[:, :, :, a, 1 - hf, :], ["wq"], ["wqr"], eng="pool")
        kb.barrier()
        A.reset(mark)
        wkv4 = wkv.rearrange("p k (h d) -> p k h d", h=8)
        mx = A.alloc([32], F32)
        kb.memset(mx, 0.0, ["mx"])
        cqr = Ring("cq", [A.alloc([3, 512], F32) for _ in range(2)])
        ckvr = Ring("ckv", [A.alloc([2, 512], F32) for _ in range(2)])
        tabr = Ring("tab", [A.alloc([4, 512], F32, parts=64) for _ in range(2)])
        sq = A.alloc([3, 512], F32)
        sqk = A.alloc([2, 512], F32)
        cqb = A.alloc([3, 512], BF16)
        ckvb = A.alloc([2, 512], BF16)
        rq = A.alloc([512], F32)
        rkv = A.alloc([512], F32)
        csr = A.alloc([2, 512], F32, parts=64)
        rtok = Ring("rtok", [A.alloc([2], F32) for _ in range(4)])
        st128 = Ring("s128", [A.alloc([512], BF16) for _ in range(3)])
        st64 = Ring("s64", [A.alloc([512], BF16, parts=64) for _ in range(3)])
        sqb = Ring("sqb", [A.alloc([512], BF16) for _ in range(3)])
        t64 = Ring("t64", [A.alloc([512], F32, parts=64) for _ in range(4)])
        vst = Ring("vst", [A.alloc([1024], BF16) for _ in range(2)])
        mtmp = Ring("mtmp", [A.alloc([1], F32) for _ in range(4)])
        psr = Ring("ps", [kb.ps[i] for i in range(8)])

        def sqmax(stage, stok, parts, col, G):
            s2, s2t = sqb.next()
            kb.act(s2[0:parts, 0:G], stage[0:parts, 0:G], AF.Square, [stok], [s2t])
            ps, pt = psr.next()
            kb.mm(ps[:, 0:G], [(ones_bf[0:parts, :], s2[0:parts, 0:G])], [s2t, "ones_bf"], [pt])
            m, mt = mtmp.next()
            kb.S.add("dve", lambda e, o=m, i=ps[:, 0:G]: e.reduce_max(out=o, in_=i, axis=AX.X), [pt], [mt])
            kb.tt(mx[:, col:col + 1], mx[:, col:col + 1], m, ALU.max, [mt, "mx"], ["mx"])

        def rstd_bc(dst, dtok, sqt, sqtok, nk, G, n):
            ps, pt = psr.next()
            kb.mm(ps[:, 0:G], [(ones_f, sqt[:, k, 0:G]) for k in range(nk)], [sqtok, "ones_f"], [pt])
            kb.act(dst[:, 0:G], ps[:, 0:G], AF.Sqrt, [pt], [dtok], bias=self.eps_col, scale=1.0 / n)
            kb.recip(dst[:, 0:G], dst[:, 0:G], [dtok], [dtok])

        VMHv = self.VMH.rearrange("h p kb d -> p kb h d")
        for (t0, G) in GROUPS:
            cq, cqt = cqr.next()
            ckv, ckvt = ckvr.next()
            tab, tabt = tabr.next()
            kb.dma(cq[:, :, 0:G], self.CQT.rearrange("(k p) t -> p k t", p=128)[:, :, t0:t0 + G], writes=[cqt])
            kb.dma(ckv[:, :, 0:G], self.CKVT.rearrange("(k p) t -> p k t", p=128)[:, :, t0:t0 + G], writes=[ckvt])
            kb.dma(tab[:, 0, 0:G], self.KRT[:, t0:t0 + G], writes=[tabt])
            kb.dma(tab[:, 1, 0:G], self.KRRT[:, t0:t0 + G], reads=[tabt], writes=[tabt])
            kb.dma(tab[:, 2:4, 0:G], io["rope"].rearrange("c p t -> p c t")[:, :, t0:t0 + G], reads=[tabt], writes=[tabt])
            kb.act(sq[:, :, 0:G], cq[:, :, 0:G], AF.Square, [cqt], ["sq"])
            kb.act(sqk[:, :, 0:G], ckv[:, :, 0:G], AF.Square, [ckvt], ["sqk"])
            kb.copy(cqb[:, :, 0:G], cq[:, :, 0:G], [cqt], ["cqb"], eng="pool")
            kb.copy(ckvb[:, :, 0:G], ckv[:, :, 0:G], [ckvt], ["ckvb"], eng="pool")
            rstd_bc(rq, "rq", sq, "sq", 3, G, 384.0)
            rstd_bc(rkv, "rkv", sqk, "sqk", 2, G, 256.0)
            kb.tt(csr[:, 0, 0:G], tab[:, 2, 0:G], rq[0:64, 0:G], ALU.mult, [tabt, "rq"], ["csr"], eng="pool")
            kb.tt(csr[:, 1, 0:G], tab[:, 3, 0:G], rq[0:64, 0:G], ALU.mult, [tabt, "rq"], ["csr"], eng="pool")
            ta, tat = t64.next()
            tb, tbt = t64.next()
            kb.tt(ta[:, 0:G], tab[:, 0, 0:G], tab[:, 2, 0:G], ALU.mult, [tabt], [tat], eng="pool")
            kb.tt(tb[:, 0:G], tab[:, 1, 0:G], tab[:, 3, 0:G], ALU.mult, [tabt], [tbt], eng="pool")
            s6, s6t = st64.next()
            kb.tt(s6[:, 0:G], ta[:, 0:G], tb[:, 0:G], ALU.add, [tat, tbt], [s6t], eng="pool")
            kb.dma(self.KRo[:, t0:t0 + G], s6[:, 0:G], [s6t], [])
            sqmax(s6, s6t, 64, 24, G)
            for h in range(8):
                ps, pt = psr.next()
                kb.mm(ps[:, 0:G], [(wq[:, k, h * 192:h * 192 + 128], cqb[:, k, 0:G]) for k in range(3)], ["wq", "cqb"], [pt])
                s1, s1t = st128.next()
                kb.tt(s1[:, 0:G], ps[:, 0:G], rq[:, 0:G], ALU.mult, [pt, "rq"], [s1t])
                kb.dma(self.QN[h, :, t0:t0 + G], s1[:, 0:G], [s1t], [])
                sqmax(s1, s1t, 128, h, G)
                psa, pta = psr.next()
                psb, ptb = psr.next()
                kb.mm(psa[0:64, 0:G], [(wq[:, k, h * 192 + 128:h * 192 + 192], cqb[:, k, 0:G]) for k in range(3)], ["wq", "cqb"], [pta])
                kb.mm(psb[0:64, 0:G], [(wqr[:, k, h, :], cqb[:, k, 0:G]) for k in range(3)], ["wqr", "cqb"], [ptb])
                ta, tat = t64.next()
                tb, tbt = t64.next()
                kb.tt(ta[:, 0:G], psa[0:64, 0:G], csr[:, 0, 0:G], ALU.mult, [pta, "csr"], [tat])
                kb.tt(tb[:, 0:G], psb[0:64, 0:G], csr[:, 1, 0:G], ALU.mult, [ptb, "csr"], [tbt])
                s6, s6t = st64.next()
                kb.tt(s6[:, 0:G], ta[:, 0:G], tb[:, 0:G], ALU.add, [tat, tbt], [s6t], eng="pool")
                kb.dma(self.QR[h, :, t0:t0 + G], s6[:, 0:G], [s6t], [])
                sqmax(s6, s6t, 64, 8 + h, G)
                ps, pt = psr.next()
                kb.mm(ps[:, 0:G], [(wkv4[:, k, h, 0:128], ckvb[:, k, 0:G]) for k in range(2)], ["wkv", "ckvb"], [pt])
                s1, s1t = st128.next()
                kb.tt(s1[:, 0:G], ps[:, 0:G], rkv[:, 0:G], ALU.mult, [pt, "rkv"], [s1t])
                kb.dma(self.KN[h, :, t0:t0 + G], s1[:, 0:G], [s1t], [])
                sqmax(s1, s1t, 128, 16 + h, G)
            for ti in range(G // 128):
                ts_ = slice(ti * 128, (ti + 1) * 128)
                ps, pt = psr.next()
                kb.mm(ps[:, 0:2], [(sqk[:, k, ts_], ones_f[:, 0:2]) for k in range(2)], ["sqk", "ones_f"], [pt])
                rt, rtt = rtok.next()
                kb.act(rt, ps[:, 0:2], AF.Sqrt, [pt], [rtt], bias=self.eps_col, scale=1.0 / 256)
                kb.recip(rt, rt, [rtt], [rtt])
                vs, vst_t = vst.next()
                for half in range(2):
                    ps, pt = psr.next()
                    kb.mm(ps, [(ckvb[:, k, ts_], wkv4[:, k, half * 4:(half + 1) * 4, 128:256]) for k in range(2)], ["wkv", "ckvb"], [pt])
                    kb.act(vs[:, half * 512:(half + 1) * 512], ps, AF.Copy, [pt, rtt], [vst_t], scale=rt[:, 0:1])
                kbi = (t0 + ti * 128) // 128
                kb.dma(VMHv[:, kbi, :, :], vs.rearrange("p (h d) -> p h d", h=8), [vst_t], [])
        mb = self.mla_bias
        tq = A.alloc([8], F32)
        tk = A.alloc([8], F32)
        kb.tt(tq, mx[:, 0:8], mx[:, 8:16], ALU.add, ["mx"], ["tq"])
        kb.ts(tk, mx[:, 16:24], mx[:, 24:25], None, ALU.add, None, ["mx"], ["tk"])
        kb.tt(tq, tq, tk, ALU.mult, ["tq", "tk"], ["tq"])
        kb.act(tq, tq, AF.Sqrt, ["tq"], ["tq"])
        kb.ts(mb, tq, -MLA_SCALE, None, ALU.mult, None, ["tq"], ["mla_bias"])
        kb.barrier()

    def phase3_attn(self, l):
        kb, A, io = self.kb, self.kb.A, self.io
        A.reset(self.base)
        ones_bf = self.ones_bf
        mb = self.mla_bias
        krt = A.alloc([T], BF16)
        kb.memset(krt[64:128, :], 0.0, ["krt"])
        kb.dma(krt[0:64, :], self.KRo, reads=["krt"], writes=["krt"])
        hbuf = []
        for i in range(2):
            hbuf.append(dict(kn=A.alloc([T], BF16), vm=A.alloc([NT, 128], BF16), qn=A.alloc([T], BF16),
                             qr=A.alloc([T], BF16)))
            kb.memset(hbuf[i]["qr"][64:128, :], 0.0, ["h%dqr" % i])
        ptr = Ring("pt", [A.alloc([512], BF16) for _ in range(4)])
        rlr = Ring("rl", [A.alloc([512], F32) for _ in range(2)])
        lar = Ring("lacc", [A.alloc([512], F32) for _ in range(2)])
        osr = Ring("os", [A.alloc([512], BF16) for _ in range(2)])
        sps = Ring("sps", [kb.ps[i] for i in range(4)])
        gi = 0
        for h in range(8):
            hb = hbuf[h % 2]
            tk = "h%d" % (h % 2)
            kb.dma(hb["kn"], self.KN[h], writes=[tk + "kn"])
            kb.dma(hb["vm"], self.VMH[h], writes=[tk + "vm"])
            kb.dma(hb["qn"], self.QN[h], writes=[tk + "qn"])
            kb.dma(hb["qr"][0:64, :], self.QR[h], reads=[tk + "qr"], writes=[tk + "qr"])
            for (t0, G) in GROUPS:
                kbs = [0, 1] if t0 == 0 else list(range(NT))
                po = kb.ps[4 + (gi % 2)]
                pot = ("ps", 4 + (gi % 2))
                pl = kb.ps[6 + (gi % 2)]
                plt = ("ps", 6 + (gi % 2))
                lacc, lat = lar.next()
                gi += 1
                def issue_S(kbi, hb=hb, tk=tk, t0=t0, G=G):
                    ks = slice(kbi * 128, (kbi + 1) * 128)
                    ps, pst = sps.next()
                    kb.mm(ps[:, 0:G], [(hb["kn"][:, ks], hb["qn"][:, t0:t0 + G]), (krt[:, ks], hb["qr"][:, t0:t0 + G])],
                          [tk + "kn", tk + "qn", "krt", tk + "qr"], [pst])
                    return ps, pst
                LOOK = 2
                Sq = [issue_S(kbs[i]) for i in range(min(LOOK, len(kbs)))]
                for n, kbi in enumerate(kbs):
                    ps, pst = Sq.pop(0)
                    pt, ptt = ptr.next()
                    kb.act(pt[:, 0:G], ps[:, 0:G], AF.Exp, [pst, "mla_bias"], [ptt], bias=mb[:, h:h + 1], scale=MLA_SCALE)
                    if n + LOOK < len(kbs):
                        Sq.append(issue_S(kbs[n + LOOK]))
                    first, last = (n == 0), (n == len(kbs) - 1)
                    kb.mm(po[:, 0:G], [(hb["vm"][:, kbi, :], pt[:, 0:G])], [tk + "vm", ptt], [pot], start=first, stop=last)
                    if first:
                        kb.copy(lacc[:, 0:G], pt[:, 0:G], [ptt], [lat])
                    else:
                        kb.tt(lacc[:, 0:G], lacc[:, 0:G], pt[:, 0:G], ALU.add, [ptt, lat], [lat])
                kb.mm(pl[:, 0:G], [(self.ones_f, lacc[:, 0:G])], ["ones_f", lat], [plt])
                rl, rlt = rlr.next()
                kb.recip(rl[:, 0:G], pl[:, 0:G], [plt], [rlt])
                os_, ost = osr.next()
                kb.tt(os_[:, 0:G], po[:, 0:G], rl[:, 0:G], ALU.mult, [pot, rlt], [ost])
                kb.dma(self.ATT[h * 128:(h + 1) * 128, t0:t0 + G], os_[:, 0:G], [ost], [])
        kb.barrier()

    def phase3_out(self, l):
        kb, A, io = self.kb, self.kb.A, self.io
        A.reset(self.base)
        wo = A.alloc([8, D], BF16)
        self.load_w_bf(wo, io["mla_wo"][l], "wo")
        atr = Ring("at", [A.alloc([8, 512], BF16) for _ in range(2)])
        gtr = Ring("gt", [A.alloc([512], F32) for _ in range(3)])
        mtr = Ring("mt", [A.alloc([512], F32) for _ in range(3)])
        tmr = Ring("tm", [A.alloc([512], F32) for _ in range(3)])
        psr = Ring("ps", [kb.ps[i] for i in range(6)])
        ATv = self.ATT.rearrange("(k p) t -> p k t", p=128)
        for (t0, G) in GROUPS:
            at, att = atr.next()
            kb.dma(at[:, :, 0:G], ATv[:, :, t0:t0 + G], writes=[att])
            for j in range(8):
                ps, pt = psr.next()
                kb.mm(ps[:, 0:G], [(wo[:, k, j * 128:(j + 1) * 128], at[:, k, 0:G]) for k in range(8)], ["wo", att], [pt])
                gt, gtt = gtr.next()
                mt, mtt = mtr.next()
                kb.dma(gt[:, 0:G], self.GATES[D + j * 128:D + (j + 1) * 128, t0:t0 + G], writes=[gtt])
                kb.dma(mt[:, 0:G], self.MT[j * 128:(j + 1) * 128, t0:t0 + G], writes=[mtt])
                tm, tmt = tmr.next()
                kb.tt(tm[:, 0:G], ps[:, 0:G], gt[:, 0:G], ALU.mult, [pt, gtt], [tmt])
                kb.tt(mt[:, 0:G], tm[:, 0:G], mt[:, 0:G], ALU.add, [tmt, mtt], [mtt], eng="pool")
                kb.dma(self.MT[j * 128:(j + 1) * 128, t0:t0 + G], mt[:, 0:G], [mtt], [])
        kb.barrier()

    def resid_ln(self, R_, ps_halves, ps_toks, xsrc_rows, gate_bc, gate_tok, g_bc, b_bc, gb_toks):
        kb = self.kb
        x, xt = R_["x"].next()
        kb.dma(x, xsrc_rows, writes=[xt])
        a, at = R_["a"].next()
        for half in range(2):
            hs = slice(half * 512, (half + 1) * 512)
            kb.tt(a[:, hs], ps_halves[half], gate_bc[:, hs], ALU.mult, [ps_toks[half], gate_tok], [at])
        kb.stt(a, x, ALU_ALPHA, a, ALU.mult, ALU.add, [xt, at], [at])
        st, stt_ = R_["st"].next()
        for half in range(2):
            kb.S.add("dve", lambda e, o=st[:, half, :], i=a[:, half * 512:(half + 1) * 512]: e.bn_stats(out=o, in_=i), [at], [stt_])
        mv, mvt = R_["mv"].next()
        kb.S.add("dve", lambda e, o=mv, i=st: e.bn_aggr(out=o, in_=i), [stt_], [mvt])
        rs, rst = R_["rs"].next()
        kb.act(rs, mv[:, 1:2], AF.Sqrt, [mvt], [rst], bias=self.eps_col)
        kb.recip(rs, rs, [rst], [rst])
        kb.ts(a, a, mv[:, 0:1], rs[:, 0:1], ALU.subtract, ALU.mult, [at, mvt, rst], [at])
        kb.tt(a, a, g_bc, ALU.mult, [at, gb_toks[0]], [at], eng="pool")
        kb.tt(x, a, b_bc, ALU.add, [at, gb_toks[1]], [xt])
        return x, xt

    def make_resid_rings(self, A):
        return dict(x=Ring("rx", [A.alloc([D], F32) for _ in range(2)]),
                    a=Ring("ra", [A.alloc([D], F32) for _ in range(2)]),
                    st=Ring("rst", [A.alloc([2, 6], F32) for _ in range(2)]),
                    mv=Ring("rmv", [A.alloc([2], F32) for _ in range(2)]),
                    rs=Ring("rrs", [A.alloc([1], F32) for _ in range(2)]))

    def load_T_small(self, dst, src2d, nrows, tok):
        kb, A = self.kb, self.kb.A
        ncol = src2d.shape[1]
        nch = ncol // 128
        tmp = A.alloc([ncol], F32, parts=nrows)
        kb.dma(tmp, src2d, writes=[tok + "_tmp"])
        for c0 in range(0, nch, 8):
            n = min(8, nch - c0)
            bank = (c0 // 8) % 2
            ps = kb.ps[bank]
            pt = ("ps", bank)
            for j in range(n):
                kb.mm(ps[:, j * nrows:(j + 1) * nrows], [(tmp[:, (c0 + j) * 128:(c0 + j + 1) * 128], self.ident_f[0:nrows, 0:nrows])],
                      [tok + "_tmp", "ident_f"], [pt])
            kb.copy(dst[:, c0:c0 + n, :], ps[:, 0:n * nrows].rearrange("p (a b) -> p a b", a=n), [pt], [tok])

    def phase4(self, l):
        kb, A, io = self.kb, self.kb.A, self.io
        A.reset(self.base)
        ident, ones_f = self.ident, self.ones_f
        Xsrc = io["xin"] if l == 0 else self.X
        wco = A.alloc([8, D], BF16)
        wout = A.alloc([8, D], BF16)
        self.load_w_bf(wco, io["conv_wo"][l], "wco")
        self.load_w_bf(wout, io["w_out"][l], "wout")
        cw = A.alloc([8, 31], F32)
        vec = A.alloc([3, 8], F32)
        for i, nm in enumerate(("conv_db", "conv_ln_g", "conv_ln_b")):
            kb.dma(vec[:, i, :], io[nm][l].rearrange("(k p) -> p k", p=128), reads=["vec"], writes=["vec"], slow=True)
        mark = A.top
        self.load_T_small(cw, io["conv_dw"][l], 31, "cw")
        kb.barrier()
        A.reset(mark)
        bc = {}
        for nm, src, c0 in (("g1_lat", self.MOD[l, 0], 2048), ("g1_ctx", self.MOD[l, 1], 2048),
                            ("sh2_lat", self.MOD[l, 0], 3072), ("sc2_lat", self.MOD[l, 0], 4096),
                            ("sh2_ctx", self.MOD[l, 1], 3072), ("sc2_ctx", self.MOD[l, 1], 4096),
                            ("lng", io["ln1_g"][l], 0), ("lnb", io["ln1_b"][l], 0)):
            bc[nm] = A.alloc([D], F32)
            self.load_bc(bc[nm], src[c0:c0 + D], "bc_" + nm)
        hbuf = A.alloc([8, 542], BF16)
        dgr = Ring("dg", [A.alloc([31, 128], BF16) for _ in range(2)])
        acc = A.alloc([8, 512], F32)
        sqr = Ring("sq", [A.alloc([512], F32) for _ in range(2)])
        mean = A.alloc([512], F32)
        msq = A.alloc([512], F32)
        rstd = A.alloc([512], F32)
        tnr = Ring("tn", [A.alloc([512], F32) for _ in range(2)])
        cn = A.alloc([8, 512], BF16)
        mg = A.alloc([8, 512], BF16)
        gtr = Ring("gt", [A.alloc([512], F32) for _ in range(2)])
        mtr = Ring("mt", [A.alloc([512], F32) for _ in range(2)])
        tmr = Ring("tm", [A.alloc([512], F32) for _ in range(2)])
        RR = self.make_resid_rings(A)
        hbr = Ring("hb2", [A.alloc([D], BF16) for _ in range(2)])
        h2s = Ring("h2s", [A.alloc([8, 512], BF16) for _ in range(2)])
        HCv = self.HC.rearrange("(k p) t -> p k t", p=128)
        H2v = self.H2T.rearrange("(k p) t -> p k t", p=128)
        psr = Ring("ps", [kb.ps[i] for i in range(2, 6)])
        for (t0, G) in GROUPS:
            seg_lo, seg_hi = (0, TC) if t0 < TC else (TC, T)
            lo, hi = max(seg_lo, t0 - 15), min(seg_hi, t0 + G + 15)
            if lo > t0 - 15:
                kb.memset(hbuf[:, :, 0:15], 0.0, ["hbuf"])
            if hi < t0 + G + 15:
                kb.memset(hbuf[:, :, 15 + G:30 + G], 0.0, ["hbuf"])
            kb.dma(hbuf[:, :, lo - (t0 - 15):hi - (t0 - 15)], HCv[:, :, lo:hi], reads=["hbuf"], writes=["hbuf"], q="pool")
            for k in range(8):
                dg, dgt = dgr.next()
                kb.tt(dg, ident.unsqueeze(1).to_broadcast([128, 31, 128]),
                      cw[:, k, :].unsqueeze(2).to_broadcast([128, 31, 128]), ALU.mult, ["ident", "cw"], [dgt])
                ps, pt = psr.next()
                kb.mm(ps[:, 0:G], [(dg[:, j, :], hbuf[:, k, j:j + G]) for j in range(31)], [dgt, "hbuf"], [pt])
                kb.act(acc[:, k, 0:G], ps[:, 0:G], AF.Identity, [pt, "vec"], [("acc", k)], bias=vec[:, 0, k:k + 1])
            p1, p2 = kb.ps[0], kb.ps[1]
            for k in range(8):
                kb.mm(p1[:, 0:G], [(ones_f, acc[:, k, 0:G])], [("acc", k), "ones_f"], [("ps", 0)], start=(k == 0), stop=(k == 7))
                s2, s2t = sqr.next()
                kb.act(s2[:, 0:G], acc[:, k, 0:G], AF.Square, [("acc", k)], [s2t])
                kb.mm(p2[:, 0:G], [(ones_f, s2[:, 0:G])], [s2t, "ones_f"], [("ps", 1)], start=(k == 0), stop=(k == 7))
            kb.act(mean[:, 0:G], p1[:, 0:G], AF.Copy, [("ps", 0)], ["mean"], scale=1.0 / D)
            kb.act(msq[:, 0:G], p1[:, 0:G], AF.Square, [("ps", 0)], ["msq"], scale=1.0 / D)
            kb.stt(rstd[:, 0:G], p2[:, 0:G], 1.0 / D, msq[:, 0:G], ALU.mult, ALU.subtract, [("ps", 1), "msq"], ["rstd"])
            kb.act(rstd[:, 0:G], rstd[:, 0:G], AF.Sqrt, ["rstd"], ["rstd"], bias=self.eps_col)
            kb.recip(rstd[:, 0:G], rstd[:, 0:G], ["rstd"], ["rstd"])
            for k in range(8):
                tn, tnt = tnr.next()
                kb.tt(tn[:, 0:G], acc[:, k, 0:G], mean[:, 0:G], ALU.subtract, [("acc", k), "mean"], [tnt])
                kb.tt(tn[:, 0:G], tn[:, 0:G], rstd[:, 0:G], ALU.mult, [tnt, "rstd"], [tnt], eng="pool")
                kb.act(cn[:, k, 0:G], tn[:, 0:G], AF.Silu, [tnt, "vec"], ["cn"], bias=vec[:, 2, k:k + 1], scale=vec[:, 1, k:k + 1])
            for j in range(8):
                ps, pt = psr.next()
                kb.mm(ps[:, 0:G], [(wco[:, k, j * 128:(j + 1) * 128], cn[:, k, 0:G]) for k in range(8)], ["wco", "cn"], [pt])
                gt, gtt = gtr.next()
                mt, mtt = mtr.next()
                kb.dma(gt[:, 0:G], self.GATES[2 * D + j * 128:2 * D + (j + 1) * 128, t0:t0 + G], writes=[gtt])
                kb.dma(mt[:, 0:G], self.MT[j * 128:(j + 1) * 128, t0:t0 + G], writes=[mtt])
                tm, tmt = tmr.next()
                kb.tt(tm[:, 0:G], ps[:, 0:G], gt[:, 0:G], ALU.mult, [pt, gtt], [tmt])
                kb.tt(mg[:, j, 0:G], tm[:, 0:G], mt[:, 0:G], ALU.add, [tmt, mtt], ["mg"], eng="pool")
            h2, h2t = h2s.next()
            for ti in range(G // 128):
                r0 = t0 + ti * 128
                sfx = "ctx" if r0 < TC else "lat"
                halves, toks = [], []
                for half in range(2):
                    ps, pt = psr.next()
                    kb.mm(ps, [(mg[:, k, ti * 128:(ti + 1) * 128], wout[:, k, half * 512:(half + 1) * 512]) for k in range(8)],
                          ["mg", "wout"], [pt])
                    halves.append(ps)
                    toks.append(pt)
                xn, xnt = self.resid_ln(RR, halves, toks, Xsrc[r0:r0 + 128, :], bc["g1_" + sfx], "bc_g1_" + sfx,
                                        bc["lng"], bc["lnb"], ("bc_lng", "bc_lnb"))
                kb.dma(self.X[r0:r0 + 128, :], xn, [xnt], [])
                hb, hbt = hbr.next()
                a, at = RR["a"].next()
                kb.tt(a, xn, bc["sc2_" + sfx], ALU.mult, [xnt, "bc_sc2_" + sfx], [at])
                kb.tt(hb, a, bc["sh2_" + sfx], ALU.add, [at, "bc_sh2_" + sfx], [hbt], eng="pool")
                bank = 6 + (ti % 2)
                pst = kb.ps[bank].bitcast(BF16)
                ptok = ("ps", bank)
                for k in range(8):
                    kb.transpose(pst[:, k * 128:(k + 1) * 128], hb[:, k * 128:(k + 1) * 128], ident, [hbt, "ident"], [ptok])
                kb.copy(h2[:, :, ti * 128:(ti + 1) * 128], pst.rearrange("p (a b) -> p a b", a=8), [ptok], [h2t], eng="act")
            kb.dma(H2v[:, :, t0:t0 + G], h2[:, :, 0:G], [h2t], [])
        kb.barrier()

    def phase6_up(self, l):
        kb, A, io = self.kb, self.kb.A, self.io
        A.reset(self.base)
        dw = A.alloc([44, 3], F32)
        db = A.alloc([44], F32)
        kb.dma(db, io["ffn_db"][l].rearrange("(k p) -> p k", p=128), writes=["db"], slow=True)
        mark = A.top
        self.load_T_small(dw, io["ffn_dw"][l], 3, "dw")
        kb.barrier()
        A.reset(mark)
        h2T = A.alloc([8, T], BF16)
        kb.dma(h2T, self.H2T.rearrange("(k p) t -> p k t", p=128), writes=["h2T"])
        W = T + 1
        bufs = [A.alloc([T + 3], F32) for _ in range(2)]
        us = [A.alloc([W], F32) for _ in range(2)]
        ob = A.alloc([W], BF16)
        wring = Ring("w", [A.alloc([8, 512], BF16) for _ in range(4)])
        wv = io["ffn_wup"][l].rearrange("(kc p) n -> p kc n", p=128)
        psr = Ring("ps", [kb.ps[i] for i in range(8)])
        for b_ in bufs:
            kb.memset(b_[:, 0:1], 0.0, ["gv0", "gv1"])
            kb.memset(b_[:, TC + 1:TC + 2], 0.0, ["gv0", "gv1"])
            kb.memset(b_[:, T + 2:T + 3], 0.0, ["gv0", "gv1"])
        for blk in range(6):
            nck = 4 if blk < 5 else 2
            ws = []
            for s in range(2):
                w, wt = wring.next()
                c0 = s * DFF + blk * 512
                kb.dma(w[:, :, 0:nck * 128], wv[:, :, c0:c0 + nck * 128], writes=[wt], q="pool")
                ws.append((w, wt))
            for j in range(nck):
                c = blk * 4 + j
                for s in range(2):
                    w, wt = ws[s]
                    for (t0, G) in GROUPS:
                        ps, pt = psr.next()
                        kb.mm(ps[:, 0:G], [(w[:, k, j * 128:(j + 1) * 128], h2T[:, k, t0:t0 + G]) for k in range(8)], [wt, "h2T"], [pt])
                        col = t0 + 1 if t0 < TC else t0 + 2
                        kb.copy(bufs[s][:, col:col + G], ps[:, 0:G], [pt], ["gv%d" % s], eng="act")
                    ch = s * 22 + c
                    kb.act(us[s], bufs[s][:, 0:W], AF.Identity, ["gv%d" % s, "dw", "db"], ["u%d" % s],
                           bias=db[:, ch:ch + 1], scale=dw[:, ch, 0:1])
                    for tap in (1, 2):
                        kb.stt(us[s], bufs[s][:, tap:tap + W], dw[:, ch, tap:tap + 1], us[s], ALU.mult, ALU.add,
                               ["gv%d" % s, "dw", "u%d" % s], ["u%d" % s])
                kb.act(us[0], us[0], AF.Silu, ["u0"], ["u0"])
                kb.tt(ob, us[0], us[1], ALU.mult, ["u0", "u1"], ["ob"])
                kb.dma(self.ACTT[c * 128:(c + 1) * 128, 0:TC], ob[:, 0:TC], ["ob"], [])
                kb.dma(self.ACTT[c * 128:(c + 1) * 128, TC:T], ob[:, TC + 1:T + 1], ["ob"], [])
        kb.barrier()

    def phase6_down(self, l, last):
        kb, A, io = self.kb, self.kb.A, self.io
        A.reset(self.base)
        wd = A.alloc([22, D], BF16)
        kb.dma(wd, io["ffn_wdown"][l].rearrange("(c p) n -> p c n", p=128), writes=["wd"], q="pool")
        bc = {}
        for nm, src, c0 in (("g2_lat", self.MOD[l, 0], 5120), ("g2_ctx", self.MOD[l, 1], 5120),
                            ("lng", io["ln2_g"][l], 0), ("lnb", io["ln2_b"][l], 0)):
            bc[nm] = A.alloc([D], F32)
            self.load_bc(bc[nm], src[c0:c0 + D], "bc_" + nm)
        atr = Ring("at", [A.alloc([22, 512], BF16) for _ in range(2)])
        RR = self.make_resid_rings(A)
        ACv = self.ACTT.rearrange("(c p) t -> p c t", p=128)
        psr = Ring("ps", [kb.ps[i] for i in range(8)])
        for (t0, G) in GROUPS:
            if last and t0 < TC:
                continue
            at, att = atr.next()
            kb.dma(at[:, :, 0:G], ACv[:, :, t0:t0 + G], writes=[att])
            for ti in range(G // 128):
                r0 = t0 + ti * 128
                sfx = "ctx" if r0 < TC else "lat"
                halves, toks = [], []
                for half in range(2):
                    ps, pt = psr.next()
                    kb.mm(ps, [(at[:, c, ti * 128:(ti + 1) * 128], wd[:, c, half * 512:(half + 1) * 512]) for c in range(22)],
                          [att, "wd"], [pt])
                    halves.append(ps)
                    toks.append(pt)
                xn, xnt = self.resid_ln(RR, halves, toks, self.X[r0:r0 + 128, :], bc["g2_" + sfx], "bc_g2_" + sfx,
                                        bc["lng"], bc["lnb"], ("bc_lng", "bc_lnb"))
                if last:
                    kb.dma(self.out[r0 - TC:r0 - TC + 128, :], xn, [xnt], [])
                else:
                    kb.dma(self.X[r0:r0 + 128, :], xn, [xnt], [])
        kb.barrier()

    def finish(self):
        kb = self.kb
        kb.S.wait_all("sp", kb.S.dma_hw[-kb.S.n_dma_sems:] + kb.S.dma_sw[-kb.S.n_dma_sems:])
        kb.S.emit()


def rope_tables():
    rows = TL // 64
    row = np.repeat(np.arange(rows, dtype=np.float32), 64)
    col = np.tile(np.arange(64, dtype=np.float32), rows)
    inv = (np.float32(10000.0) ** (np.float32(-2.0) * np.arange(16, dtype=np.float32) / np.float32(32))).astype(np.float32)
    ang = np.stack([row[:, None] * inv, col[:, None] * inv], axis=1).astype(np.float32)
    cos = np.cos(ang).astype(np.float32)
    sin = np.sin(ang).astype(np.float32)
    tab = np.zeros((2, 64, T), np.float32)
    tab[0, :, :TC] = 1.0
    for a in range(2):
        for h in range(2):
            r0 = a * 32 + h * 16
            tab[0, r0:r0 + 16, TC:] = cos[:, a, :].T
            tab[1, r0:r0 + 16, TC:] = (-sin[:, a, :].T) if h == 0 else sin[:, a, :].T
    return tab


def make_in_maps(inputs, cores):
    rope = rope_tables()
    wts = {n: np.ascontiguousarray(np.asarray(inputs[n], dtype=np.float32)) for n in W_NAMES}
    x = np.asarray(inputs["x"], dtype=np.float32)
    c = np.asarray(inputs["c"], dtype=np.float32)
    ctx = np.asarray(inputs["ctx"], dtype=np.float32)
    c_ctx = np.asarray(inputs["c_ctx"], dtype=np.float32)
    maps = []
    for b in cores:
        m = dict(wts)
        m["xin"] = np.ascontiguousarray(np.concatenate([ctx[b], x[b]], axis=0))
        cc = np.stack([c[b], c_ctx], axis=0)
        m["ccT"] = np.ascontiguousarray(cc.reshape(2, 8, 128).transpose(2, 1, 0))
        m["rope"] = rope
        maps.append(m)
    return maps


def build_full():
    P = Prog(n_layers=DEPTH)
    P.phase0()
    for l in range(DEPTH):
        P.phase1(l)
        P.phase2_prep(l)
        P.phase2_scan(l)
        P.phase2_out(l)
        P.phase3_prep(l)
        P.phase3_attn(l)
        P.phase3_out(l)
        P.phase4(l)
        P.phase6_up(l)
        P.phase6_down(l, l == DEPTH - 1)
    P.finish()
    return P


def kernel(**inputs):
    n = 8
    P = build_full()
    maps = make_in_maps(inputs, list(range(n)))
    res = run_bass_kernel_spmd(P.nc, maps, core_ids=list(range(n)))
    out = np.stack([np.asarray(res.results[b]["out"], dtype=np.float32) for b in range(n)], axis=0)
    return out
```
